# Optimizing a Trainium2 kernel written in Bass

```python
import math
import jax, jax.numpy as jnp
from jax import lax
import numpy as np

D_MODEL = 2048
BATCH = 8
SEQ = 4096
DEPTH = 4
DEC_BATCH = 2
DEC_SEQ = 8192
PAST_LEN = 128

GRID_W = 64
Q_BLOCK = 128
HEAD_DIM = 128
ROPE_THETA = 10000.0
EPS = 1e-6
A_HEADS = D_MODEL // HEAD_DIM
A_KV_HEADS = A_HEADS // 4
A_GROUP = A_HEADS // A_KV_HEADS
A_WIDTH = A_HEADS * HEAD_DIM
A_KV_WIDTH = A_KV_HEADS * HEAD_DIM
A_IN = 2 * A_WIDTH + 2 * A_KV_WIDTH
B_HEADS = D_MODEL // (2 * HEAD_DIM)
B_WIDTH = B_HEADS * 2 * HEAD_DIM
B_IN = 4 * B_WIDTH
N_A = (DEPTH + 1) // 2
N_B = DEPTH // 2

kernel_name = "hybrid_gqa_axial_diffattn_encoder"


def rmsnorm(x, g):
    xf = x.astype(jnp.float32)
    y = xf * lax.rsqrt(jnp.mean(xf * xf, axis=-1, keepdims=True) + EPS)
    return (y * g.astype(jnp.float32)).astype(x.dtype)


def rope(x, pos, dim):
    inv = 1.0 / (ROPE_THETA ** (jnp.arange(0, dim, 2, dtype=jnp.float32) / dim))
    ang = pos[:, None] * inv[None, :]
    cos = jnp.cos(ang)[None, :, None, :]
    sin = jnp.sin(ang)[None, :, None, :]
    x1, x2 = jnp.split(x.astype(jnp.float32), 2, axis=-1)
    return jnp.concatenate([x1 * cos - x2 * sin, x2 * cos + x1 * sin], axis=-1).astype(x.dtype)


def axial_rope(x, row, col):
    half = HEAD_DIM // 2
    return jnp.concatenate([rope(x[..., :half], row, half), rope(x[..., half:], col, half)], axis=-1)


def gqa_axial_layer(x, norm_g, w_in, q_g, k_g, w_out):
    B, S, _ = x.shape
    h = rmsnorm(x, norm_g)
    proj = h @ w_in
    q, k, v, z = jnp.split(proj, [A_WIDTH, A_WIDTH + A_KV_WIDTH, A_WIDTH + 2 * A_KV_WIDTH], axis=-1)
    q = rmsnorm(q.reshape(B, S, A_HEADS, HEAD_DIM), q_g)
    k = rmsnorm(k.reshape(B, S, A_KV_HEADS, HEAD_DIM), k_g)
    v = v.reshape(B, S, A_KV_HEADS, HEAD_DIM)
    n_rows = S // GRID_W
    row = jnp.repeat(jnp.arange(n_rows, dtype=jnp.float32), GRID_W)
    col = jnp.tile(jnp.arange(GRID_W, dtype=jnp.float32), n_rows)
    q = axial_rope(q, row, col) * (HEAD_DIM ** -0.5)
    k = axial_rope(k, row, col)
    nb = S // Q_BLOCK
    qb = q.reshape(B, nb, Q_BLOCK, A_KV_HEADS, A_GROUP, HEAD_DIM).transpose(1, 0, 2, 3, 4, 5)

    def block(q_blk):
        s = jnp.einsum('bqhgd,bkhd->bhgqk', q_blk, k).astype(jnp.float32)
        p = jax.nn.softmax(s, axis=-1).astype(v.dtype)
        return jnp.einsum('bhgqk,bkhd->bqhgd', p, v)

    o = lax.map(block, qb).transpose(1, 0, 2, 3, 4, 5).reshape(B, S, A_WIDTH)
    return x + (o * jax.nn.silu(z)) @ w_out


def diff_attn_layer(x, layer_idx, norm_g, w_in, q_g, k_g, lq1, lk1, lq2, lk2, subln_g, w_out):
    B, S, _ = x.shape
    lam_init = 0.8 - 0.6 * math.exp(-0.3 * layer_idx)
    h = rmsnorm(x, norm_g)
    proj = h @ w_in
    q, k, v, z = jnp.split(proj, [B_WIDTH, 2 * B_WIDTH, 3 * B_WIDTH], axis=-1)
    q = rmsnorm(q.reshape(B, S, 2 * B_HEADS, HEAD_DIM), q_g)
    k = rmsnorm(k.reshape(B, S, 2 * B_HEADS, HEAD_DIM), k_g)
    v = v.reshape(B, S, B_HEADS, 2 * HEAD_DIM)
    pos = jnp.arange(S, dtype=jnp.float32)
    q = rope(q, pos, HEAD_DIM) * (HEAD_DIM ** -0.5)
    k = rope(k, pos, HEAD_DIM)
    f32 = jnp.float32
    lam = (jnp.exp(jnp.sum(lq1.astype(f32) * lk1.astype(f32)))
           - jnp.exp(jnp.sum(lq2.astype(f32) * lk2.astype(f32))) + lam_init)
    nb = S // Q_BLOCK
    qb = q.reshape(B, nb, Q_BLOCK, 2 * B_HEADS, HEAD_DIM).transpose(1, 0, 2, 3, 4)

    def block(q_blk):
        s = jnp.einsum('bqnd,bknd->bnqk', q_blk, k).astype(f32)
        p = jax.nn.softmax(s, axis=-1).reshape(B, B_HEADS, 2, Q_BLOCK, S)
        a = (p[:, :, 0] - lam * p[:, :, 1]).astype(v.dtype)
        return jnp.einsum('bhqk,bkhe->bqhe', a, v)

    o = lax.map(block, qb).transpose(1, 0, 2, 3, 4).reshape(B, S, B_HEADS, 2 * HEAD_DIM)
    o = (rmsnorm(o, subln_g) * (1.0 - lam_init)).reshape(B, S, B_WIDTH)
    return x + (o * jax.nn.silu(z)) @ w_out


def run_trunk(x, a_norm, a_w_in, a_q_norm, a_k_norm, a_w_out,
              b_norm, b_w_in, b_q_norm, b_k_norm, b_lambda_q1, b_lambda_k1,
              b_lambda_q2, b_lambda_k2, b_subln, b_w_out):
    for i in range(DEPTH):
        j = i // 2
        if i % 2 == 0:
            x = gqa_axial_layer(x, a_norm[j], a_w_in[j], a_q_norm[j], a_k_norm[j], a_w_out[j])
        else:
            x = diff_attn_layer(x, i, b_norm[j], b_w_in[j], b_q_norm[j], b_k_norm[j],
                                b_lambda_q1[j], b_lambda_k1[j], b_lambda_q2[j], b_lambda_k2[j],
                                b_subln[j], b_w_out[j])
    return x


def setup_inputs(seed: int = 0) -> dict:
    key = jax.random.key(seed)
    ks = jax.random.split(key, 17)
    f32 = jnp.float32
    nrm = lambda k, shape, s: jax.random.normal(k, shape, f32) * s
    gain = lambda k, shape: 1.0 + 0.02 * jax.random.normal(k, shape, f32)
    return {
        "x_prompt": jax.random.normal(ks[0], (BATCH, SEQ, D_MODEL), f32),
        "x_sample": jax.random.normal(ks[1], (DEC_BATCH, DEC_SEQ, D_MODEL), f32),
        "a_norm": gain(ks[2], (N_A, D_MODEL)),
        "a_w_in": nrm(ks[3], (N_A, D_MODEL, A_IN), D_MODEL ** -0.5),
        "a_q_norm": gain(ks[4], (N_A, HEAD_DIM)),
        "a_k_norm": gain(ks[5], (N_A, HEAD_DIM)),
        "a_w_out": nrm(ks[6], (N_A, A_WIDTH, D_MODEL), A_WIDTH ** -0.5),
        "b_norm": gain(ks[7], (N_B, D_MODEL)),
        "b_w_in": nrm(ks[8], (N_B, D_MODEL, B_IN), D_MODEL ** -0.5),
        "b_q_norm": gain(ks[9], (N_B, HEAD_DIM)),
        "b_k_norm": gain(ks[10], (N_B, HEAD_DIM)),
        "b_lambda_q1": nrm(ks[11], (N_B, HEAD_DIM), 0.1),
        "b_lambda_k1": nrm(ks[12], (N_B, HEAD_DIM), 0.1),
        "b_lambda_q2": nrm(ks[13], (N_B, HEAD_DIM), 0.1),
        "b_lambda_k2": nrm(ks[14], (N_B, HEAD_DIM), 0.1),
        "b_subln": gain(ks[15], (N_B, 2 * HEAD_DIM)),
        "b_w_out": nrm(ks[16], (N_B, B_WIDTH, D_MODEL), B_WIDTH ** -0.5),
    }


def reference(x_prompt, x_sample, a_norm, a_w_in, a_q_norm, a_k_norm, a_w_out,
              b_norm, b_w_in, b_q_norm, b_k_norm, b_lambda_q1, b_lambda_k1,
              b_lambda_q2, b_lambda_k2, b_subln, b_w_out):
    y_prompt = run_trunk(x_prompt, a_norm, a_w_in, a_q_norm, a_k_norm, a_w_out,
                         b_norm, b_w_in, b_q_norm, b_k_norm, b_lambda_q1, b_lambda_k1,
                         b_lambda_q2, b_lambda_k2, b_subln, b_w_out)
    y_sample = run_trunk(x_sample, a_norm, a_w_in, a_q_norm, a_k_norm, a_w_out,
                         b_norm, b_w_in, b_q_norm, b_k_norm, b_lambda_q1, b_lambda_k1,
                         b_lambda_q2, b_lambda_k2, b_subln, b_w_out)
    return (y_prompt, y_sample)
```

```python
import math
from contextlib import ExitStack

import numpy as np
import ml_dtypes

import concourse.bass as bass
import concourse.mybir as mybir
from concourse.bass_utils import run_bass_kernel_spmd

F32 = mybir.dt.float32
BF16 = mybir.dt.bfloat16
AF = mybir.ActivationFunctionType
ALU = mybir.AluOpType

D = 2048
NCH = 16
EPS = 1e-6
NCORES = 8
A_IN = 5120
B_IN = 8192
VCOLS = 32


class _Op:
    __slots__ = ("eng", "fn", "weng", "wdma", "dma", "dmaval", "dmainc", "idx", "need_inc", "ninc")


class Sched:
    ENGS = ("pe", "act", "dve", "pool", "sp")

    def __init__(self):
        self.ops = {e: [] for e in self.ENGS}
        self.res = {}
        self.dma_cnt = {}
        self.dma_nobarrier = set()
        self.last_compute = {}

    def _add_dep(self, o, dep):
        if dep is o:
            return
        if dep.dma is not None:
            if o.wdma.get(dep.dma, 0) < dep.dmaval:
                o.wdma[dep.dma] = dep.dmaval
        else:
            if dep.eng == "pe" and o.eng == "pe" and o.dma is None:
                return
            if o.weng.get(dep.eng, -1) < dep.idx:
                o.weng[dep.eng] = dep.idx
            dep.need_inc = True

    def op(self, eng, fn, reads=(), writes=(), dma=None, ndma=1, dmainc=16, extra=()):
        o = _Op()
        o.eng = eng
        o.fn = fn
        o.weng = {}
        o.wdma = {}
        o.dma = dma
        o.need_inc = False
        o.ninc = 0
        o.dmaval = 0
        o.dmainc = dmainc
        o.idx = len(self.ops[eng])
        for k in reads:
            r = self.res.get(k)
            if r is not None and r[0] is not None:
                self._add_dep(o, r[0])
            if r is not None and k[0] == "PS":
                for rd in r[1]:
                    if rd.eng != eng:
                        self._add_dep(o, rd)
        for k in writes:
            r = self.res.get(k)
            if r is not None:
                if r[0] is not None:
                    self._add_dep(o, r[0])
                for rd in r[1]:
                    self._add_dep(o, rd)
        for d in extra:
            self._add_dep(o, d)
        for k in reads:
            r = self.res.get(k)
            if r is None:
                self.res[k] = [None, [o]]
            else:
                r[1].append(o)
        for k in writes:
            self.res[k] = [o, []]
        if dma is not None:
            c = self.dma_cnt.get(dma, 0) + dmainc * ndma
            self.dma_cnt[dma] = c
            o.dmaval = c
        else:
            self.last_compute[eng] = o
        self.ops[eng].append(o)
        return o

    def barrier(self):
        lasts = [v for v in self.last_compute.values()]
        dmas = {k: v for k, v in self.dma_cnt.items() if k not in self.dma_nobarrier}
        for e in self.ENGS:
            o = _Op()
            o.eng = e
            o.fn = None
            o.weng = {}
            o.wdma = dict(dmas)
            o.dma = None
            o.need_inc = False
            o.ninc = 0
            o.dmaval = 0
            o.dmainc = 0
            o.idx = len(self.ops[e])
            for l in lasts:
                if l.eng != e:
                    if o.weng.get(l.eng, -1) < l.idx:
                        o.weng[l.eng] = l.idx
                    l.need_inc = True
            self.ops[e].append(o)
        self.res = {}

    def emit(self, nc, stack):
        esem = {e: stack.enter_context(nc.semaphore("e_" + e)) for e in ("pe", "act", "dve", "pool")}
        dsem = {k: stack.enter_context(nc.semaphore("d_" + k)) for k in self.dma_cnt}
        for e in self.ENGS:
            n = 0
            for o in self.ops[e]:
                if o.need_inc and o.dma is None and o.fn is not None:
                    n += 1
                o.ninc = n
        block = stack.enter_context(nc.Block())
        ops = self.ops

        def run(ename, eng):
            seen = {}
            seend = {}
            for o in ops[ename]:
                for f, idx in o.weng.items():
                    tgt = ops[f][idx]
                    val = tgt.ninc
                    if seen.get(f, 0) < val:
                        eng.wait_ge(esem[f], val)
                        seen[f] = val
                for k, val in o.wdma.items():
                    if seend.get(k, 0) < val:
                        eng.wait_ge(dsem[k], val)
                        seend[k] = val
                if o.fn is None:
                    continue
                r = o.fn(eng)
                if o.dma is not None:
                    if not isinstance(r, (list, tuple)):
                        r = [r]
                    for ins in r:
                        if o.dmainc == 16:
                            ins.then_inc(dsem[o.dma], 16)
                        else:
                            ins.then_inc(dsem[o.dma])
                elif o.need_inc:
                    r.then_inc(esem[ename], 1)

        @block.tensor
        def _(eng):
            run("pe", eng)

        @block.scalar
        def _(eng):
            run("act", eng)

        @block.vector
        def _(eng):
            run("dve", eng)

        @block.gpsimd
        def _(eng):
            run("pool", eng)

        @block.sync
        def _(eng):
            run("sp", eng)


class Prog:
    def __init__(self, cfg):
        self.cfg = cfg
        self.TP = cfg["TP"]
        self.TS = cfg["TS"]
        self.KT = cfg["KT"]
        self.depth = cfg["depth"]
        self.T = self.TP + 2 * self.TS
        self.NB = self.T // 512
        assert self.T % 512 == 0 and self.TP % 512 == 0 and self.TS % 512 == 0
        assert self.TP % self.KT == 0 and self.KT % self.TS == 0 or self.TS % self.KT == 0
        self.s = Sched()
        self.nc = bass.Bass("TRN2", target_bir_lowering=False)
        self.uid = 0

    def dram_in(self, name, shape, dt):
        return self.nc.dram_tensor(name, list(shape), dt, kind="ExternalInput").ap()

    def dram_out(self, name, shape, dt):
        return self.nc.dram_tensor(name, list(shape), dt, kind="ExternalOutput").ap()

    def dram_tmp(self, name, shape, dt):
        return self.nc.dram_tensor(name, list(shape), dt, kind="Internal").ap()

    def sb(self, name, shape, dt):
        return self.stack.enter_context(self.nc.sbuf_tensor(name, list(shape), dt))

    def dma(self, q, sem, out, in_, reads=(), writes=(), extra=()):
        return self.s.op(q, lambda e, out=out, in_=in_: e.dma_start(out=out, in_=in_),
                         reads=reads, writes=writes, dma=sem, extra=extra)

    def mm(self, out, lhsT, rhs, start, stop, reads, writes):
        return self.s.op("pe", lambda e: e.matmul(out, lhsT, rhs, start=start, stop=stop),
                         reads=reads, writes=writes)

    def act(self, out, in_, func, reads, writes, bias=None, scale=None):
        kw = {}
        if bias is not None:
            kw["bias"] = bias
        if scale is not None:
            kw["scale"] = scale
        return self.s.op("act", lambda e: e.activation(out, in_, func, **kw), reads=reads, writes=writes)

    def tt(self, eng, out, in0, in1, op, reads, writes):
        return self.s.op(eng, lambda e: e.tensor_tensor(out, in0, in1, op), reads=reads, writes=writes)

    def ts(self, eng, out, in0, s1, s2, op0, op1, reads, writes):
        if op1 is None:
            return self.s.op(eng, lambda e: e.tensor_scalar(out, in0, s1, None, op0), reads=reads, writes=writes)
        return self.s.op(eng, lambda e: e.tensor_scalar(out, in0, s1, s2, op0, op1), reads=reads, writes=writes)

    def stt(self, out, in0, scalar, in1, op0, op1, reads, writes):
        return self.s.op("dve", lambda e: e.scalar_tensor_tensor(out, in0, scalar, in1, op0, op1),
                         reads=reads, writes=writes)

    def cp(self, eng, out, in_, reads, writes):
        return self.s.op(eng, lambda e: e.tensor_copy(out, in_), reads=reads, writes=writes)

    def build(self):
        nc = self.nc
        T, TP, TS, KT = self.T, self.TP, self.TS, self.KT
        depth = self.depth
        with ExitStack() as stack:
            self.stack = stack
            self.xin = self.dram_in("xin", [T, D], F32)
            self.y = self.dram_out("y", [T, D], F32)
            self.w = {
                "a_in": self.dram_in("a_w_in", [2, D, A_IN], F32),
                "a_out": self.dram_in("a_w_out", [2, D, D], F32),
                "b_in": self.dram_in("b_w_in", [2, D, B_IN], F32),
                "b_out": self.dram_in("b_w_out", [2, D, D], F32),
            }
            self.vecs_d = self.dram_in("vecs", [128, 4 * VCOLS], F32)
            self.rope_d = self.dram_in("rope", [2, 2, 128, T], F32)
            self.ident_d = self.dram_in("ident", [128, 128], F32)
            self.rmat_d = self.dram_in("rmat", [2, 128, 128], BF16)

            self.xT = self.dram_tmp("xT", [NCH, 128, T], F32)
            self.wb_in = self.dram_tmp("wb_in", [D, B_IN], BF16)
            self.wb_out = self.dram_tmp("wb_out", [D, D], BF16)
            self.qS = self.dram_tmp("qS", [16, 128, T], BF16)
            self.zS = self.dram_tmp("zS", [16, 128, T], BF16)
            self.aS = self.dram_tmp("aS", [16, 128, T], BF16)
            self.kvp = self.dram_tmp("kvp", [2, 16, 128, TP], BF16)
            self.kvsrc = {
                0: self.dram_tmp("kvsrcA", [2 * 4 * 128, 2 * TS], BF16),
                1: self.dram_tmp("kvsrcB", [2 * 16 * 128, 2 * TS], BF16),
            }
            self.gath = {
                0: self.dram_tmp("gathA", [NCORES * 2 * 4 * 128, 2 * TS], BF16),
                1: self.dram_tmp("gathB", [NCORES * 2 * 16 * 128, 2 * TS], BF16),
            }

            self.XF = self.sb("XF", [128, 16, 512], F32)
            self.HB = self.sb("HB", [128, 16, 512], BF16)
            self.GB = [self.sb("GB%d" % i, [128, 16, 512], BF16) for i in range(3)]
            self.TF = self.sb("TF", [128, 20, 512], F32)
            self.SB = self.sb("SBq", [128, 16, 512], BF16)
            self.VS = self.sb("VS", [128, 4, 2048], BF16)
            self.ones_bf = self.sb("ones_bf", [128, 128], BF16)
            self.ones_f = self.sb("ones_f", [128, 128], F32)
            self.ident = self.sb("ident_s", [128, 128], F32)
            self.rmat = self.sb("rmat_s", [128, 2, 128], BF16)
            self.vecs = self.sb("vecs_s", [128, 4 * VCOLS], F32)
            self.epsc = self.sb("epsc", [128, 1], F32)
            self.dv = self.sb("dv", [128, 8], F32)
            self.PS = [stack.enter_context(nc.psum_tensor("ps%d" % i, [128, 512], F32)) for i in range(8)]

            self.setup()
            self.phase0()
            if self.cfg.get("stop", 9) >= 0:
                for li in range(depth):
                    self.layer(li)
            self.s.barrier()
            self.s.emit(nc, stack)
        return nc

    def setup(self):
        s = self.s
        s.op("dve", lambda e: e.memset(self.ones_bf[:], 1.0), writes=[("c", 0)])
        s.op("dve", lambda e: e.memset(self.ones_f[:], 1.0), writes=[("c", 1)])
        s.op("dve", lambda e: e.memset(self.epsc[:], EPS), writes=[("c", 2)])
        self.dma("sp", "cst", self.ident[:], self.ident_d, writes=[("c", 3)])
        self.dma("sp", "cst", self.rmat[:], self.rmat_d.rearrange("k p m -> p k m"), writes=[("c", 4)])
        self.dma("sp", "cst", self.vecs[:], self.vecs_d, writes=[("c", 5)])
        s.barrier()

    def phase0(self):
        s = self.s
        YS = self.TF
        cnt = 0
        for b in range(self.NB):
            tok0 = b * 512
            for j in range(4):
                ys = j % 2
                ysl = YS[:, ys * 4:(ys + 1) * 4, :].rearrange("p a b -> p (a b)")
                self.dma("sp", "ld_ys%d" % ys, ysl, self.xin[tok0 + j * 128: tok0 + (j + 1) * 128, :],
                         writes=[("YS", ys)])
                for fg in range(4):
                    ps = self.PS[cnt % 4]
                    for i in range(4):
                        f = fg * 4 + i
                        s.op("pe", lambda e, ps=ps, i=i, f=f, ysl=ysl: e.transpose(
                            ps[:, i * 128:(i + 1) * 128], ysl[:, f * 128:(f + 1) * 128], self.ident[:]),
                            reads=[("YS", ys)], writes=[("PS", cnt % 4)])
                    eng = "dve" if cnt % 2 == 0 else "act"
                    outap = self.XF[:, fg * 4:(fg + 1) * 4, j * 128:(j + 1) * 128]
                    inap = ps[:, :].rearrange("p (a b) -> p a b", a=4)
                    if eng == "dve":
                        self.cp("dve", outap, inap, reads=[("PS", cnt % 4)], writes=[("XF", fg)])
                    else:
                        self.act(outap, inap, AF.Copy, reads=[("PS", cnt % 4)], writes=[("XF", fg)])
                    cnt += 1
            self.dma("sp", "st_x", self.xT[:, :, tok0:tok0 + 512].rearrange("c p t -> p c t"), self.XF[:],
                     reads=[("XF", g) for g in range(4)])
        s.barrier()

    def layer(self, li):
        kind = li % 2
        j = li // 2
        self.kind = kind
        self.li = li
        self.vb = li * VCOLS
        self.NU = 4 if kind == 0 else 16
        self.NIN = A_IN if kind == 0 else B_IN
        self.w_in = self.w["a_in" if kind == 0 else "b_in"][j]
        self.w_out = self.w["a_out" if kind == 0 else "b_out"][j]
        self.lam_init = 0.8 - 0.6 * math.exp(-0.3 * li)
        stop = self.cfg.get("stop", 9)
        self.phaseW()
        if kind == 1:
            self.prepB()
        if stop >= 1:
            self.phase1()
        if stop >= 2:
            self.phase2()
        if stop >= 3:
            self.phase3(last=(li == self.depth - 1))

    def phaseW(self):
        s = self.s
        cnt = 0
        jobs = []
        for kc in range(NCH):
            c0 = 0
            while c0 < self.NIN:
                w = min(2048, self.NIN - c0)
                jobs.append((self.w_in, self.wb_in, kc, c0, w, True))
                c0 += w
        for kc in range(NCH):
            jobs.append((self.w_out, self.wb_out, kc, 0, 2048, False))
        for (src, dst, kc, c0, w, scaled) in jobs:
            slot = cnt % 2
            fin = self.TF[:, slot * 4:(slot + 1) * 4, :].rearrange("p a b -> p (a b)")[:, 0:w]
            bout = self.SB[:, slot * 4:(slot + 1) * 4, :].rearrange("p a b -> p (a b)")[:, 0:w]
            self.dma("sp", "ld_w%d" % slot, fin, src[kc * 128:(kc + 1) * 128, c0:c0 + w], writes=[("WF", slot)])
            eng = "dve" if cnt % 2 == 0 else "pool"
            if scaled:
                gcol = self.vecs[:, self.vb + kc: self.vb + kc + 1]
                self.ts(eng, bout, fin, gcol, None, ALU.mult, None, reads=[("WF", slot)], writes=[("WB", slot)])
            else:
                self.cp(eng, bout, fin, reads=[("WF", slot)], writes=[("WB", slot)])
            self.dma("sp", "st_w%d" % slot, dst[kc * 128:(kc + 1) * 128, c0:c0 + w], bout, reads=[("WB", slot)])
            cnt += 1
        s.barrier()

    def prepB(self):
        s = self.s
        vb = self.vb
        dv = self.dv
        v = self.vecs
        self.tt("dve", dv[:, 0:1], v[:, vb + 20:vb + 21], v[:, vb + 21:vb + 22], ALU.mult, reads=[], writes=[("dv", 0)])
        self.tt("dve", dv[:, 1:2], v[:, vb + 22:vb + 23], v[:, vb + 23:vb + 24], ALU.mult, reads=[("dv", 0)], writes=[("dv", 1)])
        ps = self.PS[7]
        self.mm(ps[:, 0:2], self.ones_f[:], dv[:, 0:2], True, True, reads=[("dv", 0), ("dv", 1)], writes=[("PS", 7)])
        self.act(dv[:, 2:4], ps[:, 0:2], AF.Exp, reads=[("PS", 7)], writes=[("dv", 2)])
        self.stt(dv[:, 4:5], dv[:, 3:4], -self.lam_init, dv[:, 2:3], ALU.add, ALU.subtract,
                 reads=[("dv", 2)], writes=[("dv", 4)])
        self.ts("dve", dv[:, 5:7], v[:, vb + 24:vb + 26], 1.0 - self.lam_init, None, ALU.mult, None,
                reads=[("dv", 4)], writes=[("dv", 5)])
        s.barrier()

    def phase1(self):
        s = self.s
        kind, NU, T, TP, TS = self.kind, self.NU, self.T, self.TP, self.TS
        vb = self.vb
        XF, HB, TF, SB, VS, PS = self.XF, self.HB, self.TF, self.SB, self.VS, self.PS
        nog = self.NIN // 512
        if kind == 0:
            og_kind = ["q"] * 4 + ["k", "v"] + ["z"] * 4
        else:
            og_kind = ["q"] * 4 + ["k"] * 4 + ["v"] * 4 + ["z"] * 4
        first = {}
        for i, k in enumerate(og_kind):
            first.setdefault(k, i)
        sample_blocks = [b for b in range(self.NB) if b * 512 >= TP]
        prompt_blocks = [b for b in range(self.NB) if b * 512 < TP]
        order = sample_blocks + prompt_blocks
        qscale = 128.0 ** -0.5
        self.wcnt = 0
        self.ucnt = 0
        self.pcnt = 0
        self.ocnt = 0
        kvsrc = self.kvsrc[kind].rearrange("(a u p) t -> a u p t", a=2, u=NU)
        ag_op = None
        kv_sample_keys = []
        for bi, b in enumerate(order):
            tok0 = b * 512
            is_sample = tok0 >= TP
            self.dma("sp", "ld_x", XF[:], self.xT[:, :, tok0:tok0 + 512].rearrange("c p t -> p c t"),
                     writes=[("XF", g) for g in range(4)])
            tp = bi % 2
            self.dma("sp", "ld_rc%d" % tp, TF[:, tp, :], self.rope_d[kind, 0, :, tok0:tok0 + 512], writes=[("TF", tp)])
            self.dma("sp", "ld_rs%d" % tp, TF[:, 2 + tp, :], self.rope_d[kind, 1, :, tok0:tok0 + 512], writes=[("TF", 2 + tp)])
            for g in range(4):
                self.act(HB[:, g * 4:(g + 1) * 4, :], XF[:, g * 4:(g + 1) * 4, :], AF.Square,
                         reads=[("XF", g)], writes=[("HB", g)])
            for c in range(NCH):
                self.mm(PS[4][:], self.ones_bf[:], HB[:, c, :], c == 0, c == NCH - 1,
                        reads=[("HB", c // 4)], writes=[("PS", 4)])
            self.act(TF[:, 18, :], PS[4][:], AF.Ln, reads=[("PS", 4)], writes=[("TF", 18)], bias=self.epsc[:], scale=1.0 / D)
            self.act(TF[:, 19, :], TF[:, 18, :], AF.Exp, reads=[("TF", 18)], writes=[("TF", 19)], scale=-0.5)
            for g in range(4):
                self.tt("dve", HB[:, g * 4:(g + 1) * 4, :], XF[:, g * 4:(g + 1) * 4, :],
                        TF[:, 19:20, :].to_broadcast([128, 4, 512]), ALU.mult,
                        reads=[("XF", g), ("TF", 19)], writes=[("HB", g)])
            vv = self.vecs
            self.ts("dve", TF[:, 4, :], TF[:, tp, :], vv[:, vb + 16:vb + 17], qscale, ALU.mult, ALU.mult,
                    reads=[("TF", tp)], writes=[("TF", 4)])
            self.ts("dve", TF[:, 5, :], TF[:, 2 + tp, :], vv[:, vb + 17:vb + 18], qscale, ALU.mult, ALU.mult,
                    reads=[("TF", 2 + tp)], writes=[("TF", 5)])
            self.ts("pool", TF[:, 6, :], TF[:, tp, :], vv[:, vb + 18:vb + 19], None, ALU.mult, None,
                    reads=[("TF", tp)], writes=[("TF", 6)])
            self.ts("pool", TF[:, 7, :], TF[:, 2 + tp, :], vv[:, vb + 19:vb + 20], None, ALU.mult, None,
                    reads=[("TF", 2 + tp)], writes=[("TF", 7)])
            pending = []

            def flush():
                while pending:
                    pending.pop(0)()

            for og in range(nog):
                ok = og_kind[og]
                gslot = self.wcnt % 3
                self.wcnt += 1
                W = self.GB[gslot]
                self.dma("sp", "ld_g%d" % gslot, W[:],
                         self.wb_in[:, og * 512:(og + 1) * 512].rearrange("(c p) n -> p c n", p=128),
                         writes=[("GB", gslot)])
                if ok == "v":
                    vg = og - first["v"]
                    for jj in range(4):
                        pb = self.pcnt % 4
                        self.pcnt += 1
                        for c in range(NCH):
                            self.mm(PS[pb][:], HB[:, c, jj * 128:(jj + 1) * 128], W[:, c, :], c == 0, c == NCH - 1,
                                    reads=[("HB", c // 4), ("GB", gslot)], writes=[("PS", pb)])
                        eng = "dve" if jj % 2 == 0 else "act"
                        outap = VS[:, jj, vg * 512:(vg + 1) * 512]
                        if eng == "dve":
                            self.cp("dve", outap, PS[pb][:], reads=[("PS", pb)], writes=[("VS", vg)])
                        else:
                            self.act(outap, PS[pb][:], AF.Copy, reads=[("PS", pb)], writes=[("VS", vg)])
                        flush()
                    continue
                for f in range(4):
                    pb = self.pcnt % 4
                    self.pcnt += 1
                    for c in range(NCH):
                        self.mm(PS[pb][:], W[:, c, f * 128:(f + 1) * 128], HB[:, c, :], c == 0, c == NCH - 1,
                                reads=[("HB", c // 4), ("GB", gslot)], writes=[("PS", pb)])
                    u = (og - first[ok]) * 4 + f
                    if ok == "z":
                        oslot = 4 + self.ocnt % 4
                        self.ocnt += 1
                        self.act(SB[:, oslot, :], PS[pb][:], AF.Silu, reads=[("PS", pb)], writes=[("SB", oslot)])
                        self.dma("sp", "st_o%d" % oslot, self.zS[u, :, tok0:tok0 + 512], SB[:, oslot, :],
                                 reads=[("SB", oslot)])
                        flush()
                        continue
                    i2 = self.ucnt % 2
                    self.ucnt += 1
                    is_q = ok == "q"
                    Ct = TF[:, 4 if is_q else 6, :]
                    St = TF[:, 5 if is_q else 7, :]
                    self.act(SB[:, i2, :], PS[pb][:], AF.Square, reads=[("PS", pb)], writes=[("SB", i2)])
                    self.act(SB[:, 2 + i2, :], PS[pb][:], AF.Copy, reads=[("PS", pb)], writes=[("SB", 2 + i2)])
                    self.tt("dve", TF[:, 8 + i2, :], PS[pb][:], Ct, ALU.mult,
                            reads=[("PS", pb), ("TF", 4 if is_q else 6)], writes=[("TF", 8 + i2)])
                    flush()

                    def stage1(i2=i2, is_q=is_q, u=u, St=St, tok0=tok0, is_sample=is_sample):
                        m1, m2 = 4 + 2 * i2, 5 + 2 * i2
                        self.mm(PS[m1][:], self.ones_bf[:], SB[:, i2, :], True, True,
                                reads=[("SB", i2)], writes=[("PS", m1)])
                        self.mm(PS[m2][:], self.rmat[:, self.kind, :], SB[:, 2 + i2, :], True, True,
                                reads=[("SB", 2 + i2)], writes=[("PS", m2)])
                        self.act(TF[:, 10 + i2, :], PS[m1][:], AF.Ln, reads=[("PS", m1)], writes=[("TF", 10 + i2)],
                                 bias=self.epsc[:], scale=1.0 / 128)
                        self.act(TF[:, 12 + i2, :], TF[:, 10 + i2, :], AF.Exp, reads=[("TF", 10 + i2)],
                                 writes=[("TF", 12 + i2)], scale=-0.5)
                        self.tt("dve", TF[:, 14 + i2, :], PS[m2][:], St, ALU.mult,
                                reads=[("PS", m2), ("TF", 5 if is_q else 7)], writes=[("TF", 14 + i2)])
                        self.tt("pool", TF[:, 16 + i2, :], TF[:, 8 + i2, :], TF[:, 14 + i2, :], ALU.add,
                                reads=[("TF", 8 + i2), ("TF", 14 + i2)], writes=[("TF", 16 + i2)])
                        oslot = 4 + self.ocnt % 4
                        self.ocnt += 1
                        self.tt("pool", SB[:, oslot, :], TF[:, 16 + i2, :], TF[:, 12 + i2, :], ALU.mult,
                                reads=[("TF", 16 + i2), ("TF", 12 + i2)], writes=[("SB", oslot)])
                        if is_q:
                            self.dma("sp", "st_o%d" % oslot, self.qS[u, :, tok0:tok0 + 512], SB[:, oslot, :],
                                     reads=[("SB", oslot)])
                        elif is_sample:
                            t0 = tok0 - TP
                            self.dma("sp", "st_o%d" % oslot, kvsrc[0, u, :, t0:t0 + 512], SB[:, oslot, :],
                                     reads=[("SB", oslot)], writes=[("kvs", "k", u, t0)])
                            kv_sample_keys.append(("kvs", "k", u, t0))
                        else:
                            self.dma("sp", "st_o%d" % oslot, self.kvp[0, u, :, tok0:tok0 + 512], SB[:, oslot, :],
                                     reads=[("SB", oslot)])

                    pending.append(stage1)
            flush()
            nvg = NU // 4
            for u in range(NU):
                src = VS[:, :, u * 128:(u + 1) * 128]
                if is_sample:
                    t0 = tok0 - TP
                    dst = kvsrc[1, u, :, t0:t0 + 512].rearrange("p (j d) -> p j d", j=4)
                    self.dma("sp", "st_v", dst, src, reads=[("VS", g) for g in range(nvg)],
                             writes=[("kvs", "v", u, t0)])
                    kv_sample_keys.append(("kvs", "v", u, t0))
                else:
                    dst = self.kvp[1, u, :, tok0:tok0 + 512].rearrange("p (j d) -> p j d", j=4)
                    self.dma("sp", "st_v", dst, src, reads=[("VS", g) for g in range(nvg)])
            if bi == len(sample_blocks) - 1:
                semname = "ag%d" % self.li
                self.s.dma_nobarrier.add(semname)
                src2 = self.kvsrc[kind]
                dst2 = self.gath[kind]
                ag_op = self.s.op(
                    "pool",
                    lambda e, src2=src2, dst2=dst2: e.collective_compute(
                        "AllGather", ALU.bypass, replica_groups=[list(range(NCORES))], ins=[src2], outs=[dst2]),
                    reads=list(kv_sample_keys), writes=[("gath",)], dma=semname, ndma=1, dmainc=1)
        self.ag_op = ag_op
        s.barrier()

    def phase2(self):
        s = self.s
        kind, NU, T, TP, TS, KT = self.kind, self.NU, self.T, self.TP, self.TS, self.KT
        PS, SB, TF, XF, HB = self.PS, self.SB, self.TF, self.XF, self.HB
        NCK = KT // 128
        KB = [self.GB[0][:].rearrange("p a b -> p (a b)")[:, i * KT:(i + 1) * KT] for i in range(8192 // KT)][:2]
        vflat = [self.GB[1][:].rearrange("p a b -> p (a b)"), self.GB[2][:].rearrange("p a b -> p (a b)")]
        VB = []
        for fl in vflat:
            for i in range(min(2, 8192 // KT)):
                VB.append(fl[:, i * KT:(i + 1) * KT].rearrange("p (c d) -> p c d", d=128))
        nkb, nvb = len(KB), len(VB)
        gath = self.gath[kind].rearrange("(r a u p) t -> r a u p t", r=NCORES, a=2, u=NU)
        segs = [("P", 0, TP, TP // KT)]
        nr = max(1, KT // TS)
        ntile_s = (NCORES * TS) // KT
        segs.append(("S0", TP, TS, ntile_s))
        segs.append(("S1", TP + TS, TS, ntile_s))
        kcache = {}
        vcache = {}
        st = {"kn": 0, "vn": 0, "i": 0, "unit": 0}
        deferred = []

        def load_tile(which, seg, u, kt):
            cache = kcache if which == 0 else vcache
            tag = (seg[0], u, kt)
            for slot, tg in cache.items():
                if tg == tag:
                    return slot
            if which == 0:
                slot = st["kn"] % nkb
                st["kn"] += 1
                dst = KB[slot]
            else:
                slot = st["vn"] % nvb
                st["vn"] += 1
                dst = VB[slot]
            cache[slot] = tag
            key = ("KB" if which == 0 else "VB", slot)
            extra = ()
            if seg[0] == "P":
                src = self.kvp[which, u, :, kt * KT:(kt + 1) * KT]
                if which == 1:
                    src = src.rearrange("p (c d) -> p c d", d=128)
            else:
                sidx = 0 if seg[0] == "S0" else 1
                extra = (self.ag_op,)
                if KT >= TS:
                    r0 = kt * nr
                    src = gath[r0:r0 + nr, which, u, :, sidx * TS:(sidx + 1) * TS].rearrange("r p t -> p r t")
                    if which == 0:
                        dst = dst.rearrange("p (r t) -> p r t", r=nr)
                    else:
                        dst = dst.rearrange("p (r c) d -> p r (c d)", r=nr)
                else:
                    per = TS // KT
                    r0 = kt // per
                    o0 = sidx * TS + (kt % per) * KT
                    src = gath[r0, which, u, :, o0:o0 + KT]
                    if which == 1:
                        src = src.rearrange("p (c d) -> p c d", d=128)
            self.dma("sp", "ld_%s%d" % ("k" if which == 0 else "v", slot), dst, src, writes=[key], extra=extra)
            return slot

        def attn_unit(seg, ku, vus, qu, qb, fin):
            un = st["unit"]
            st["unit"] += 1
            oset = un % 2
            qtok = seg[1] + qb * 512
            qslot = un % 2
            self.dma("sp", "ld_q%d" % qslot, SB[:, qslot, :], self.qS[qu, :, qtok:qtok + 512], writes=[("SB", qslot)])
            nchunks = seg[3] * NCK
            pvq = []
            ci = 0
            for kt in range(seg[3]):
                ks = load_tile(0, seg, ku, kt)
                vss = [load_tile(1, seg, vu, kt) for vu in vus]
                for c in range(NCK):
                    i = st["i"]
                    st["i"] += 1
                    sb_ = i % 3
                    pslot = 8 + i % 4
                    self.mm(PS[sb_][:], KB[ks][:, c * 128:(c + 1) * 128], SB[:, qslot, :], True, True,
                            reads=[("KB", ks), ("SB", qslot)], writes=[("PS", sb_)])
                    self.act(SB[:, pslot, :], PS[sb_][:], AF.Exp, reads=[("PS", sb_)], writes=[("SB", pslot)])
                    aslot = oset * 2 + ci % 2
                    if ci < 2:
                        self.cp("dve", TF[:, aslot, :], SB[:, pslot, :], reads=[("SB", pslot)], writes=[("TF", aslot)])
                    else:
                        self.tt("dve", TF[:, aslot, :], TF[:, aslot, :], SB[:, pslot, :], ALU.add,
                                reads=[("SB", pslot), ("TF", aslot)], writes=[("TF", aslot)])

                    def pv(ci=ci, pslot=pslot, vss=vss, c=c):
                        for vi, vs in enumerate(vss):
                            ob = 3 + oset * 2 + vi
                            self.mm(PS[ob][:], VB[vs][:, c, :], SB[:, pslot, :], ci == 0, ci == nchunks - 1,
                                    reads=[("VB", vs), ("SB", pslot)], writes=[("PS", ob)])

                    pvq.append(pv)
                    if len(pvq) > 2:
                        pvq.pop(0)()
                    if ci == 3:
                        while deferred:
                            deferred.pop(0)()
                    ci += 1
            while pvq:
                pvq.pop(0)()
            a0, a1 = oset * 2, oset * 2 + 1
            accs = 4 + oset
            self.tt("dve", TF[:, accs, :], TF[:, a0, :], TF[:, a1, :], ALU.add,
                    reads=[("TF", a0), ("TF", a1)], writes=[("TF", accs)])
            deferred.append(lambda: fin(oset, accs, qtok))

        def fin_common(oset, accs):
            self.mm(PS[7][:], self.ones_f[:], TF[:, accs, :], True, True, reads=[("TF", accs)], writes=[("PS", 7)])
            rs = 6 + oset
            s.op("dve", lambda e: e.reciprocal(TF[:, rs, :], PS[7][:]), reads=[("PS", 7)], writes=[("TF", rs)])
            return rs

        zcnt = {"n": 0}

        def finA(h):
            def fin(oset, accs, qtok):
                rs = fin_common(oset, accs)
                zs = 12 + zcnt["n"] % 2
                asl = 14 + zcnt["n"] % 2
                zcnt["n"] += 1
                self.dma("sp", "ld_z%d" % zs, SB[:, zs, :], self.zS[h, :, qtok:qtok + 512], writes=[("SB", zs)])
                ob = 3 + oset * 2
                self.tt("dve", TF[:, 8 + oset, :], PS[ob][:], TF[:, rs, :], ALU.mult,
                        reads=[("PS", ob), ("TF", rs)], writes=[("TF", 8 + oset)])
                self.tt("pool", SB[:, asl, :], TF[:, 8 + oset, :], SB[:, zs, :], ALU.mult,
                        reads=[("TF", 8 + oset), ("SB", zs)], writes=[("SB", asl)])
                self.dma("sp", "st_a%d" % asl, self.aS[h, :, qtok:qtok + 512], SB[:, asl, :], reads=[("SB", asl)])
            return fin

        def finB(h, comp):
            def fin(oset, accs, qtok):
                rs = fin_common(oset, accs)
                if comp == 0:
                    for vi in range(2):
                        ob = 3 + oset * 2 + vi
                        self.tt("dve", XF[:, vi, :], PS[ob][:], TF[:, rs, :], ALU.mult,
                                reads=[("PS", ob), ("TF", rs)], writes=[("XF", ("s", vi))])
                    return
                zsl = []
                for vi in range(2):
                    zs = 4 + zcnt["n"] % 4
                    zcnt["n"] += 1
                    zsl.append(zs)
                    self.dma("sp", "ld_z%d" % zs, SB[:, zs, :], self.zS[2 * h + vi, :, qtok:qtok + 512],
                             writes=[("SB", zs)])
                for vi in range(2):
                    ob = 3 + oset * 2 + vi
                    self.tt("dve", TF[:, 8 + vi, :], PS[ob][:], TF[:, rs, :], ALU.mult,
                            reads=[("PS", ob), ("TF", rs)], writes=[("TF", 8 + vi)])
                    self.stt(XF[:, 2 + vi, :], TF[:, 8 + vi, :], self.dv[:, 4:5], XF[:, vi, :], ALU.mult, ALU.add,
                             reads=[("TF", 8 + vi), ("XF", ("s", vi))], writes=[("XF", ("d", vi))])
                    self.tt("pool", SB[:, 2 + vi, :], XF[:, 2 + vi, :], XF[:, 2 + vi, :], ALU.mult,
                            reads=[("XF", ("d", vi))], writes=[("SB", 2 + vi)])
                self.mm(PS[7][:], self.ones_bf[:], SB[:, 2, :], True, False, reads=[("SB", 2)], writes=[("PS", 7)])
                self.mm(PS[7][:], self.ones_bf[:], SB[:, 3, :], False, True, reads=[("SB", 3)], writes=[("PS", 7)])
                self.act(TF[:, 10, :], PS[7][:], AF.Ln, reads=[("PS", 7)], writes=[("TF", 10)],
                         bias=self.epsc[:], scale=1.0 / 256)
                self.act(TF[:, 11, :], TF[:, 10, :], AF.Exp, reads=[("TF", 10)], writes=[("TF", 11)], scale=-0.5)
                for vi in range(2):
                    self.tt("pool", TF[:, 12 + vi, :], XF[:, 2 + vi, :], TF[:, 11, :], ALU.mult,
                            reads=[("XF", ("d", vi)), ("TF", 11)], writes=[("TF", 12 + vi)])
                    asl = 14 + vi
                    self.stt(SB[:, asl, :], TF[:, 12 + vi, :], self.dv[:, 5 + vi:6 + vi], SB[:, zsl[vi], :],
                             ALU.mult, ALU.mult, reads=[("TF", 12 + vi), ("SB", zsl[vi])], writes=[("SB", asl)])
                    self.dma("sp", "st_a%d" % asl, self.aS[2 * h + vi, :, qtok:qtok + 512], SB[:, asl, :],
                             reads=[("SB", asl)])
            return fin

        for seg in segs:
            nqb = seg[2] // 512
            if kind == 0:
                for g in range(4):
                    for hq in range(4):
                        h = 4 * g + hq
                        for qb in range(nqb):
                            attn_unit(seg, g, [g], h, qb, finA(h))
            else:
                for h in range(8):
                    for qb in range(nqb):
                        for comp in range(2):
                            attn_unit(seg, 2 * h + comp, [2 * h, 2 * h + 1], 2 * h + comp, qb, finB(h, comp))
        while deferred:
            deferred.pop(0)()
        s.barrier()

    def phase3(self, last):
        s = self.s
        XF, HB, PS, TF = self.XF, self.HB, self.PS, self.TF
        T = self.T
        wc = 0
        pc = 0
        tcnt = 0
        for b in range(self.NB):
            tok0 = b * 512
            self.dma("sp", "ld_x", XF[:], self.xT[:, :, tok0:tok0 + 512].rearrange("c p t -> p c t"),
                     writes=[("XF", g) for g in range(16)])
            self.dma("sp", "ld_a", HB[:], self.aS[:, :, tok0:tok0 + 512].rearrange("c p t -> p c t"),
                     writes=[("HB", 0)])
            for og in range(4):
                gslot = wc % 3
                wc += 1
                W = self.GB[gslot]
                self.dma("sp", "ld_g%d" % gslot, W[:],
                         self.wb_out[:, og * 512:(og + 1) * 512].rearrange("(c p) n -> p c n", p=128),
                         writes=[("GB", gslot)])
                for f in range(4):
                    fo = og * 4 + f
                    pb = pc % 4
                    pc += 1
                    for c in range(NCH):
                        self.mm(PS[pb][:], W[:, c, f * 128:(f + 1) * 128], HB[:, c, :], c == 0, c == NCH - 1,
                                reads=[("HB", 0), ("GB", gslot)], writes=[("PS", pb)])
                    self.tt("dve", XF[:, fo, :], PS[pb][:], XF[:, fo, :], ALU.add,
                            reads=[("PS", pb), ("XF", fo)], writes=[("XF", fo)])
            if not last:
                self.dma("sp", "st_x", self.xT[:, :, tok0:tok0 + 512].rearrange("c p t -> p c t"), XF[:],
                         reads=[("XF", g) for g in range(16)])
            else:
                for j in range(4):
                    ys = j % 2
                    ysl = TF[:, ys * 4:(ys + 1) * 4, :].rearrange("p a b -> p (a b)")
                    for fg in range(4):
                        pb = 4 + tcnt % 4
                        ps = PS[pb]
                        for i in range(4):
                            fo = fg * 4 + i
                            s.op("pe", lambda e, ps=ps, i=i, fo=fo, j=j: e.transpose(
                                ps[:, i * 128:(i + 1) * 128], XF[:, fo, j * 128:(j + 1) * 128], self.ident[:]),
                                reads=[("XF", fo)], writes=[("PS", pb)])
                        eng = "dve" if tcnt % 2 == 0 else "act"
                        outap = ysl[:, fg * 512:(fg + 1) * 512]
                        if eng == "dve":
                            self.cp("dve", outap, ps[:], reads=[("PS", pb)], writes=[("YS", ys)])
                        else:
                            self.act(outap, ps[:], AF.Copy, reads=[("PS", pb)], writes=[("YS", ys)])
                        tcnt += 1
                    self.dma("sp", "st_y%d" % ys, self.y[tok0 + j * 128: tok0 + (j + 1) * 128, :], ysl,
                             reads=[("YS", ys)])
        s.barrier()


def _rope_tables(TP, TS, core):
    T = TP + 2 * TS
    pos = np.concatenate([np.arange(TP), core * TS + np.arange(TS), core * TS + np.arange(TS)]).astype(np.float32)
    out = np.zeros((2, 2, 128, T), np.float32)
    inv64 = (1.0 / (10000.0 ** (np.arange(0, 64, 2, dtype=np.float32) / 64))).astype(np.float32)
    row = np.floor(pos / 64.0).astype(np.float32)
    col = (pos - row * 64.0).astype(np.float32)
    angr = (row[None, :] * inv64[:, None]).astype(np.float32)
    angc = (col[None, :] * inv64[:, None]).astype(np.float32)
    out[0, 0, 0:32] = np.cos(angr); out[0, 0, 32:64] = np.cos(angr)
    out[0, 0, 64:96] = np.cos(angc); out[0, 0, 96:128] = np.cos(angc)
    out[0, 1, 0:32] = np.sin(angr); out[0, 1, 32:64] = np.sin(angr)
    out[0, 1, 64:96] = np.sin(angc); out[0, 1, 96:128] = np.sin(angc)
    inv128 = (1.0 / (10000.0 ** (np.arange(0, 128, 2, dtype=np.float32) / 128))).astype(np.float32)
    ang = (pos[None, :] * inv128[:, None]).astype(np.float32)
    out[1, 0, 0:64] = np.cos(ang); out[1, 0, 64:128] = np.cos(ang)
    out[1, 1, 0:64] = np.sin(ang); out[1, 1, 64:128] = np.sin(ang)
    return out


def _rmats():
    r = np.zeros((2, 128, 128), np.float32)
    for base in (0, 64):
        for m in range(32):
            r[0, base + m + 32, base + m] = -1.0
            r[0, base + m, base + m + 32] = 1.0
    for m in range(64):
        r[1, m + 64, m] = -1.0
        r[1, m, m + 64] = 1.0
    return r.astype(ml_dtypes.bfloat16)


def _perm(kind):
    p = np.zeros(128, np.int64)
    if kind == 0:
        for base in (0, 64):
            for m in range(32):
                p[base + m] = base + m + 32
                p[base + m + 32] = base + m
    else:
        for m in range(64):
            p[m] = m + 64
            p[m + 64] = m
    return p


def _vecs(inp, depth):
    v = np.zeros((128, 4 * VCOLS), np.float32)
    for li in range(depth):
        kind, j = li % 2, li // 2
        b = li * VCOLS
        pre = "a_" if kind == 0 else "b_"
        v[:, b:b + 16] = np.asarray(inp[pre + "norm"][j], np.float32).reshape(16, 128).T
        gq = np.asarray(inp[pre + "q_norm"][j], np.float32)
        gk = np.asarray(inp[pre + "k_norm"][j], np.float32)
        pm = _perm(kind)
        v[:, b + 16] = gq
        v[:, b + 17] = gq[pm]
        v[:, b + 18] = gk
        v[:, b + 19] = gk[pm]
        if kind == 1:
            v[:, b + 20] = np.asarray(inp["b_lambda_q1"][j], np.float32)
            v[:, b + 21] = np.asarray(inp["b_lambda_k1"][j], np.float32)
            v[:, b + 22] = np.asarray(inp["b_lambda_q2"][j], np.float32)
            v[:, b + 23] = np.asarray(inp["b_lambda_k2"][j], np.float32)
            gs = np.asarray(inp["b_subln"][j], np.float32)
            v[:, b + 24] = gs[0:128]
            v[:, b + 25] = gs[128:256]
    return v


_PROG_CACHE = {}


def _get_prog(cfg):
    key = tuple(sorted(cfg.items()))
    if key not in _PROG_CACHE:
        _PROG_CACHE[key] = Prog(dict(cfg)).build()
    return _PROG_CACHE[key]


def run_cfg(inp, cfg, trace=False):
    TP, TS, depth = cfg["TP"], cfg["TS"], cfg["depth"]
    xp = np.asarray(inp["x_prompt"], np.float32)
    xs = np.asarray(inp["x_sample"], np.float32)
    assert xp.shape == (NCORES, TP, D) and xs.shape == (2, NCORES * TS, D)
    nc = _get_prog(cfg)
    vecs = _vecs(inp, depth)
    rm = _rmats()
    ident = np.eye(128, dtype=np.float32)
    common = {
        "a_w_in": np.ascontiguousarray(np.asarray(inp["a_w_in"], np.float32)),
        "a_w_out": np.ascontiguousarray(np.asarray(inp["a_w_out"], np.float32)),
        "b_w_in": np.ascontiguousarray(np.asarray(inp["b_w_in"], np.float32)),
        "b_w_out": np.ascontiguousarray(np.asarray(inp["b_w_out"], np.float32)),
        "vecs": vecs, "ident": ident, "rmat": rm,
    }
    in_maps = []
    for c in range(NCORES):
        xin = np.concatenate([xp[c], xs[0, c * TS:(c + 1) * TS], xs[1, c * TS:(c + 1) * TS]], axis=0)
        m = dict(common)
        m["xin"] = np.ascontiguousarray(xin)
        m["rope"] = _rope_tables(TP, TS, c)
        in_maps.append(m)
    res = run_bass_kernel_spmd(nc, in_maps, core_ids=list(range(NCORES)), trace=trace)
    yp = np.zeros((NCORES, TP, D), np.float32)
    ys = np.zeros((2, NCORES * TS, D), np.float32)
    for c in range(NCORES):
        y = np.asarray(res.results[c]["y"], np.float32)
        yp[c] = y[0:TP]
        ys[0, c * TS:(c + 1) * TS] = y[TP:TP + TS]
        ys[1, c * TS:(c + 1) * TS] = y[TP + TS:TP + 2 * TS]
    return (yp, ys), res


def kernel(**inputs):
    cfg = {"TP": 4096, "TS": 1024, "KT": 4096, "depth": 4}
    out, _ = run_cfg(inputs, cfg)
    return out
```

```python
import math
from contextlib import ExitStack

import numpy as np
import ml_dtypes

import concourse.bass as bass
import concourse.mybir as mybir
from concourse.bass_utils import run_bass_kernel_spmd

F32 = mybir.dt.float32
BF16 = mybir.dt.bfloat16
AF = mybir.ActivationFunctionType
ALU = mybir.AluOpType

D = 2048
NCH = 16
EPS = 1e-6
NCORES = 8
A_IN = 5120
B_IN = 8192
VCOLS = 32


class _Op:
    __slots__ = ("eng", "fn", "weng", "wdma", "dma", "dmaval", "dmainc", "idx", "need_inc", "ninc")


class Sched:
    ENGS = ("pe", "act", "dve", "pool", "sp")

    def __init__(self):
        self.ops = {e: [] for e in self.ENGS}
        self.res = {}
        self.dma_cnt = {}
        self.dma_nobarrier = set()
        self.last_compute = {}

    def _add_dep(self, o, dep):
        if dep is o:
            return
        if dep.dma is not None:
            if o.wdma.get(dep.dma, 0) < dep.dmaval:
                o.wdma[dep.dma] = dep.dmaval
        else:
            if dep.eng == "pe" and o.eng == "pe" and o.dma is None:
                return
            if o.weng.get(dep.eng, -1) < dep.idx:
                o.weng[dep.eng] = dep.idx
            dep.need_inc = True

    def op(self, eng, fn, reads=(), writes=(), dma=None, ndma=1, dmainc=16, extra=()):
        o = _Op()
        o.eng = eng
        o.fn = fn
        o.weng = {}
        o.wdma = {}
        o.dma = dma
        o.need_inc = False
        o.ninc = 0
        o.dmaval = 0
        o.dmainc = dmainc
        o.idx = len(self.ops[eng])
        for k in reads:
            r = self.res.get(k)
            if r is not None and r[0] is not None:
                self._add_dep(o, r[0])
            if r is not None and k[0] == "PS":
                for rd in r[1]:
                    if rd.eng != eng:
                        self._add_dep(o, rd)
        for k in writes:
            r = self.res.get(k)
            if r is not None:
                if r[0] is not None:
                    self._add_dep(o, r[0])
                for rd in r[1]:
                    self._add_dep(o, rd)
        for d in extra:
            self._add_dep(o, d)
        for k in reads:
            r = self.res.get(k)
            if r is None:
                self.res[k] = [None, [o]]
            else:
                r[1].append(o)
        for k in writes:
            self.res[k] = [o, []]
        if dma is not None:
            c = self.dma_cnt.get(dma, 0) + dmainc * ndma
            self.dma_cnt[dma] = c
            o.dmaval = c
        else:
            self.last_compute[eng] = o
        self.ops[eng].append(o)
        return o

    def barrier(self):
        lasts = [v for v in self.last_compute.values()]
        dmas = {k: v for k, v in self.dma_cnt.items() if k not in self.dma_nobarrier}
        for e in self.ENGS:
            o = _Op()
            o.eng = e
            o.fn = None
            o.weng = {}
            o.wdma = dict(dmas)
            o.dma = None
            o.need_inc = False
            o.ninc = 0
            o.dmaval = 0
            o.dmainc = 0
            o.idx = len(self.ops[e])
            for l in lasts:
                if l.eng != e:
                    if o.weng.get(l.eng, -1) < l.idx:
                        o.weng[l.eng] = l.idx
                    l.need_inc = True
            self.ops[e].append(o)
        self.res = {}

    def emit(self, nc, stack):
        esem = {e: stack.enter_context(nc.semaphore("e_" + e)) for e in ("pe", "act", "dve", "pool")}
        dsem = {k: stack.enter_context(nc.semaphore("d_" + k)) for k in self.dma_cnt}
        for e in self.ENGS:
            n = 0
            for o in self.ops[e]:
                if o.need_inc and o.dma is None and o.fn is not None:
                    n += 1
                o.ninc = n
        block = stack.enter_context(nc.Block())
        ops = self.ops

        def run(ename, eng):
            seen = {}
            seend = {}
            for o in ops[ename]:
                for f, idx in o.weng.items():
                    tgt = ops[f][idx]
                    val = tgt.ninc
                    if seen.get(f, 0) < val:
                        eng.wait_ge(esem[f], val)
                        seen[f] = val
                for k, val in o.wdma.items():
                    if seend.get(k, 0) < val:
                        eng.wait_ge(dsem[k], val)
                        seend[k] = val
                if o.fn is None:
                    continue
                r = o.fn(eng)
                if o.dma is not None:
                    if not isinstance(r, (list, tuple)):
                        r = [r]
                    for ins in r:
                        if o.dmainc == 16:
                            ins.then_inc(dsem[o.dma], 16)
                        else:
                            ins.then_inc(dsem[o.dma])
                elif o.need_inc:
                    r.then_inc(esem[ename], 1)

        @block.tensor
        def _(eng):
            run("pe", eng)

        @block.scalar
        def _(eng):
            run("act", eng)

        @block.vector
        def _(eng):
            run("dve", eng)

        @block.gpsimd
        def _(eng):
            run("pool", eng)

        @block.sync
        def _(eng):
            run("sp", eng)


class Prog:
    def __init__(self, cfg):
        self.cfg = cfg
        self.TP = cfg["TP"]
        self.TS = cfg["TS"]
        self.KT = cfg["KT"]
        self.depth = cfg["depth"]
        self.T = self.TP + 2 * self.TS
        self.NB = self.T // 512
        assert self.T % 512 == 0 and self.TP % 512 == 0 and self.TS % 512 == 0
        assert self.TP % self.KT == 0 and self.KT % self.TS == 0 or self.TS % self.KT == 0
        self.s = Sched()
        self.nc = bass.Bass("TRN2", target_bir_lowering=False)
        self.uid = 0

    def dram_in(self, name, shape, dt):
        return self.nc.dram_tensor(name, list(shape), dt, kind="ExternalInput").ap()

    def dram_out(self, name, shape, dt):
        return self.nc.dram_tensor(name, list(shape), dt, kind="ExternalOutput").ap()

    def dram_tmp(self, name, shape, dt):
        return self.nc.dram_tensor(name, list(shape), dt, kind="Internal").ap()

    def sb(self, name, shape, dt):
        return self.stack.enter_context(self.nc.sbuf_tensor(name, list(shape), dt))

    def dma(self, q, sem, out, in_, reads=(), writes=(), extra=()):
        return self.s.op(q, lambda e, out=out, in_=in_: e.dma_start(out=out, in_=in_),
                         reads=reads, writes=writes, dma=sem, extra=extra)

    def mm(self, out, lhsT, rhs, start, stop, reads, writes):
        return self.s.op("pe", lambda e: e.matmul(out, lhsT, rhs, start=start, stop=stop),
                         reads=reads, writes=writes)

    def act(self, out, in_, func, reads, writes, bias=None, scale=None):
        kw = {}
        if bias is not None:
            kw["bias"] = bias
        if scale is not None:
            kw["scale"] = scale
        return self.s.op("act", lambda e: e.activation(out, in_, func, **kw), reads=reads, writes=writes)

    def tt(self, eng, out, in0, in1, op, reads, writes):
        return self.s.op(eng, lambda e: e.tensor_tensor(out, in0, in1, op), reads=reads, writes=writes)

    def ts(self, eng, out, in0, s1, s2, op0, op1, reads, writes):
        if op1 is None:
            return self.s.op(eng, lambda e: e.tensor_scalar(out, in0, s1, None, op0), reads=reads, writes=writes)
        return self.s.op(eng, lambda e: e.tensor_scalar(out, in0, s1, s2, op0, op1), reads=reads, writes=writes)

    def stt(self, out, in0, scalar, in1, op0, op1, reads, writes):
        return self.s.op("dve", lambda e: e.scalar_tensor_tensor(out, in0, scalar, in1, op0, op1),
                         reads=reads, writes=writes)

    def cp(self, eng, out, in_, reads, writes):
        return self.s.op(eng, lambda e: e.tensor_copy(out, in_), reads=reads, writes=writes)

    def build(self):
        nc = self.nc
        T, TP, TS, KT = self.T, self.TP, self.TS, self.KT
        depth = self.depth
        with ExitStack() as stack:
            self.stack = stack
            self.xin = self.dram_in("xin", [T, D], F32)
            self.y = self.dram_out("y", [T, D], F32)
            self.w = {
                "a_in": self.dram_in("a_w_in", [2, D, A_IN], F32),
                "a_out": self.dram_in("a_w_out", [2, D, D], F32),
                "b_in": self.dram_in("b_w_in", [2, D, B_IN], F32),
                "b_out": self.dram_in("b_w_out", [2, D, D], F32),
            }
            self.vecs_d = self.dram_in("vecs", [128, 4 * VCOLS], F32)
            self.rope_d = self.dram_in("rope", [2, 2, 128, T], F32)
            self.ident_d = self.dram_in("ident", [128, 128], F32)
            self.rmat_d = self.dram_in("rmat", [2, 128, 128], BF16)

            self.xT = self.dram_tmp("xT", [NCH, 128, T], F32)
            self.wb_in = self.dram_tmp("wb_in", [16, 128, NCH, 512], BF16)
            self.wb_out = self.dram_tmp("wb_out", [4, 128, NCH, 512], BF16)
            self.qS = self.dram_tmp("qS", [16, 128, T], BF16)
            self.zS = self.dram_tmp("zS", [16, 128, T], BF16)
            self.aS = self.dram_tmp("aS", [16, 128, T], BF16)
            self.kvp = self.dram_tmp("kvp", [2, 16, 128, TP], BF16)
            self.kvsrc = {
                0: self.dram_tmp("kvsrcA", [2 * 4 * 128, 2 * TS], BF16),
                1: self.dram_tmp("kvsrcB", [2 * 16 * 128, 2 * TS], BF16),
            }
            self.gath = {
                0: self.dram_tmp("gathA", [NCORES * 2 * 4 * 128, 2 * TS], BF16),
                1: self.dram_tmp("gathB", [NCORES * 2 * 16 * 128, 2 * TS], BF16),
            }

            self.XF = self.sb("XF", [128, 16, 512], F32)
            self.HB = self.sb("HB", [128, 16, 512], BF16)
            self.HB2 = self.sb("HB2", [128, 16, 512], BF16)
            self.GB = [self.sb("GB%d" % i, [128, 16, 512], BF16) for i in range(3)]
            self.TF = self.sb("TF", [128, 20, 512], F32)
            self.SB = self.sb("SBq", [128, 16, 512], BF16)
            self.VS = self.sb("VS", [128, 4, 2048], BF16)
            self.ones_bf = self.sb("ones_bf", [128, 128], BF16)
            self.ones_f = self.sb("ones_f", [128, 128], F32)
            self.ident = self.sb("ident_s", [128, 128], F32)
            self.rmat = self.sb("rmat_s", [128, 2, 128], BF16)
            self.vecs = self.sb("vecs_s", [128, 4 * VCOLS], F32)
            self.epsc = self.sb("epsc", [128, 1], F32)
            self.dv = self.sb("dv", [128, 8], F32)
            self.PS = [stack.enter_context(nc.psum_tensor("ps%d" % i, [128, 512], F32)) for i in range(8)]

            self.setup()
            self.phase0()
            if self.cfg.get("stop", 9) >= 0:
                for li in range(depth):
                    self.layer(li)
            self.s.barrier()
            self.s.emit(nc, stack)
        return nc

    def setup(self):
        s = self.s
        s.op("dve", lambda e: e.memset(self.ones_bf[:], 1.0), writes=[("c", 0)])
        s.op("dve", lambda e: e.memset(self.ones_f[:], 1.0), writes=[("c", 1)])
        s.op("dve", lambda e: e.memset(self.epsc[:], EPS), writes=[("c", 2)])
        self.dma("sp", "cst", self.ident[:], self.ident_d, writes=[("c", 3)])
        self.dma("sp", "cst", self.rmat[:], self.rmat_d.rearrange("k p m -> p k m"), writes=[("c", 4)])
        self.dma("sp", "cst", self.vecs[:], self.vecs_d, writes=[("c", 5)])
        s.barrier()

    def phase0(self):
        s = self.s
        YS = self.TF
        cnt = 0
        for b in range(self.NB):
            tok0 = b * 512
            for j in range(4):
                ys = j % 2
                ysl = YS[:, ys * 4:(ys + 1) * 4, :].rearrange("p a b -> p (a b)")
                self.dma("sp", "ld_ys%d" % ys, ysl, self.xin[tok0 + j * 128: tok0 + (j + 1) * 128, :],
                         writes=[("YS", ys)])
                for fg in range(4):
                    ps = self.PS[cnt % 4]
                    for i in range(4):
                        f = fg * 4 + i
                        s.op("pe", lambda e, ps=ps, i=i, f=f, ysl=ysl: e.transpose(
                            ps[:, i * 128:(i + 1) * 128], ysl[:, f * 128:(f + 1) * 128], self.ident[:]),
                            reads=[("YS", ys)], writes=[("PS", cnt % 4)])
                    eng = "dve" if cnt % 2 == 0 else "act"
                    outap = self.XF[:, fg * 4:(fg + 1) * 4, j * 128:(j + 1) * 128]
                    inap = ps[:, :].rearrange("p (a b) -> p a b", a=4)
                    if eng == "dve":
                        self.cp("dve", outap, inap, reads=[("PS", cnt % 4)], writes=[("XF", fg)])
                    else:
                        self.act(outap, inap, AF.Copy, reads=[("PS", cnt % 4)], writes=[("XF", fg)])
                    cnt += 1
            self.dma("sp", "st_x", self.xT[:, :, tok0:tok0 + 512].rearrange("c p t -> p c t"), self.XF[:],
                     reads=[("XF", g) for g in range(4)])
        s.barrier()

    def layer(self, li):
        kind = li % 2
        j = li // 2
        self.kind = kind
        self.li = li
        self.vb = li * VCOLS
        self.NU = 4 if kind == 0 else 16
        self.NIN = A_IN if kind == 0 else B_IN
        self.w_in = self.w["a_in" if kind == 0 else "b_in"][j]
        self.w_out = self.w["a_out" if kind == 0 else "b_out"][j]
        self.lam_init = 0.8 - 0.6 * math.exp(-0.3 * li)
        stop = self.cfg.get("stop", 9)
        self.phaseW()
        if kind == 1:
            self.prepB()
        if stop >= 1:
            self.phase1()
        if stop >= 2:
            self.phase2()
        if stop >= 3:
            self.phase3(last=(li == self.depth - 1))

    def phaseW(self):
        s = self.s
        cnt = 0
        jobs = []
        for kc in range(NCH):
            c0 = 0
            while c0 < self.NIN:
                w = min(2048, self.NIN - c0)
                jobs.append((self.w_in, self.wb_in, kc, c0, w, True))
                c0 += w
        for kc in range(NCH):
            jobs.append((self.w_out, self.wb_out, kc, 0, 2048, False))
        for (src, dst, kc, c0, w, scaled) in jobs:
            slot = cnt % 4
            fin = self.TF[:, slot * 4:(slot + 1) * 4, :].rearrange("p a b -> p (a b)")[:, 0:w]
            bout = self.SB[:, slot * 4:(slot + 1) * 4, :].rearrange("p a b -> p (a b)")[:, 0:w]
            self.dma("sp", "ld_w%d" % slot, fin, src[kc * 128:(kc + 1) * 128, c0:c0 + w], writes=[("WF", slot)])
            if scaled:
                gcol = self.vecs[:, self.vb + kc: self.vb + kc + 1]
                self.ts("dve", bout, fin, gcol, None, ALU.mult, None, reads=[("WF", slot)], writes=[("WB", slot)])
            else:
                self.act(bout, fin, AF.Copy, reads=[("WF", slot)], writes=[("WB", slot)])
            g0, g1 = c0 // 512, (c0 + w) // 512
            self.dma("sp", "st_w%d" % slot, dst[g0:g1, :, kc, :].rearrange("g p n -> p g n"),
                     bout.rearrange("p (g n) -> p g n", n=512), reads=[("WB", slot)])
            cnt += 1
        s.barrier()

    def prepB(self):
        s = self.s
        vb = self.vb
        dv = self.dv
        v = self.vecs
        self.tt("dve", dv[:, 0:1], v[:, vb + 20:vb + 21], v[:, vb + 21:vb + 22], ALU.mult, reads=[], writes=[("dv", 0)])
        self.tt("dve", dv[:, 1:2], v[:, vb + 22:vb + 23], v[:, vb + 23:vb + 24], ALU.mult, reads=[("dv", 0)], writes=[("dv", 1)])
        ps = self.PS[7]
        SBh, SBl = self.SB[:, 2, 0:2], self.SB[:, 3, 0:2]
        self.cp("dve", SBh, dv[:, 0:2], reads=[("dv", 0), ("dv", 1)], writes=[("SBh",)])
        self.tt("dve", SBl, dv[:, 0:2], SBh, ALU.subtract, reads=[("dv", 0), ("dv", 1), ("SBh",)], writes=[("SBl",)])
        self.mm(ps[:, 0:2], self.ones_bf[:], SBh, True, False, reads=[("SBh",)], writes=[("PS", 7)])
        self.mm(ps[:, 0:2], self.ones_bf[:], SBl, False, True, reads=[("SBl",)], writes=[("PS", 7)])
        self.act(dv[:, 2:4], ps[:, 0:2], AF.Exp, reads=[("PS", 7)], writes=[("dv", 2)])
        self.stt(dv[:, 4:5], dv[:, 3:4], -self.lam_init, dv[:, 2:3], ALU.add, ALU.subtract,
                 reads=[("dv", 2)], writes=[("dv", 4)])
        self.ts("dve", dv[:, 5:7], v[:, vb + 24:vb + 26], 1.0 - self.lam_init, None, ALU.mult, None,
                reads=[("dv", 4)], writes=[("dv", 5)])
        s.barrier()

    def phase1(self):
        s = self.s
        kind, NU, T, TP, TS = self.kind, self.NU, self.T, self.TP, self.TS
        vb = self.vb
        XF, TF, SB, VS, PS = self.XF, self.TF, self.SB, self.VS, self.PS
        HBs = [self.HB, self.HB2]
        nog = self.NIN // 512
        if kind == 0:
            og_kind = ["q"] * 4 + ["k", "v"] + ["z"] * 4
        else:
            og_kind = ["q"] * 4 + ["k"] * 4 + ["v"] * 4 + ["z"] * 4
        first = {}
        for i, k in enumerate(og_kind):
            first.setdefault(k, i)
        sample_blocks = [b for b in range(self.NB) if b * 512 >= TP]
        prompt_blocks = [b for b in range(self.NB) if b * 512 < TP]
        order = sample_blocks + prompt_blocks
        qscale = 128.0 ** -0.5
        self.ucnt = 0
        self.pcnt = 0
        self.ocnt = 0
        kvsrc = self.kvsrc[kind].rearrange("(a u p) t -> a u p t", a=2, u=NU)
        ag_op = None
        kv_sample_keys = []
        vv = self.vecs

        wseq = [(bi, og) for bi in range(len(order)) for og in range(nog)]
        wissued = {"n": 0}

        def issue_w(upto):
            while wissued["n"] <= upto and wissued["n"] < len(wseq):
                n = wissued["n"]
                og = wseq[n][1]
                gslot = n % 3
                self.dma("sp", "ld_g%d" % gslot, self.GB[gslot][:], self.wb_in[og], writes=[("GB", gslot)])
                wissued["n"] += 1

        def norm_load(bi):
            b = order[bi]
            tok0 = b * 512
            tp = bi % 2
            self.dma("sp", "ld_x", XF[:], self.xT[:, :, tok0:tok0 + 512].rearrange("c p t -> p c t"),
                     writes=[("XF", g) for g in range(4)])
            self.dma("sp", "ld_rc%d" % tp, TF[:, tp, :], self.rope_d[kind, 0, :, tok0:tok0 + 512], writes=[("TF", tp)])
            self.dma("sp", "ld_rs%d" % tp, TF[:, 2 + tp, :], self.rope_d[kind, 1, :, tok0:tok0 + 512], writes=[("TF", 2 + tp)])

        def norm_square(bi):
            HB = HBs[bi % 2]
            hk = "HB%d" % (bi % 2)
            for g in range(4):
                self.act(HB[:, g * 4:(g + 1) * 4, :], XF[:, g * 4:(g + 1) * 4, :], AF.Square,
                         reads=[("XF", g)], writes=[(hk, g)])

        def norm_finish(bi):
            HB = HBs[bi % 2]
            hk = "HB%d" % (bi % 2)
            for c in range(NCH):
                self.mm(PS[4][:], self.ones_bf[:], HB[:, c, :], c == 0, c == NCH - 1,
                        reads=[(hk, c // 4)], writes=[("PS", 4)])
            self.act(TF[:, 18, :], PS[4][:], AF.Ln, reads=[("PS", 4)], writes=[("TF", 18)], bias=self.epsc[:], scale=1.0 / D)
            self.act(TF[:, 19, :], TF[:, 18, :], AF.Exp, reads=[("TF", 18)], writes=[("TF", 19)], scale=-0.5)
            for g in range(4):
                self.tt("dve", HB[:, g * 4:(g + 1) * 4, :], XF[:, g * 4:(g + 1) * 4, :],
                        TF[:, 19:20, :].to_broadcast([128, 4, 512]), ALU.mult,
                        reads=[("XF", g), ("TF", 19)], writes=[(hk, g)])

        def norm_stage(bi):
            norm_load(bi)
            norm_square(bi)
            norm_finish(bi)

        issue_w(1)
        norm_stage(0)
        wn = 0
        for bi, b in enumerate(order):
            tok0 = b * 512
            is_sample = tok0 >= TP
            HB = HBs[bi % 2]
            hk = "HB%d" % (bi % 2)
            tp = bi % 2
            self.ts("dve", TF[:, 4, :], TF[:, tp, :], vv[:, vb + 16:vb + 17], qscale, ALU.mult, ALU.mult,
                    reads=[("TF", tp)], writes=[("TF", 4)])
            self.ts("dve", TF[:, 5, :], TF[:, 2 + tp, :], vv[:, vb + 17:vb + 18], qscale, ALU.mult, ALU.mult,
                    reads=[("TF", 2 + tp)], writes=[("TF", 5)])
            self.ts("dve", TF[:, 6, :], TF[:, tp, :], vv[:, vb + 18:vb + 19], None, ALU.mult, None,
                    reads=[("TF", tp)], writes=[("TF", 6)])
            self.ts("dve", TF[:, 7, :], TF[:, 2 + tp, :], vv[:, vb + 19:vb + 20], None, ALU.mult, None,
                    reads=[("TF", 2 + tp)], writes=[("TF", 7)])
            pending = []

            def flush():
                while pending:
                    pending.pop(0)()

            for og in range(nog):
                ok = og_kind[og]
                gslot = wn % 3
                issue_w(wn + 2)
                wn += 1
                W = self.GB[gslot]
                if bi + 1 < len(order):
                    if og == 0:
                        norm_load(bi + 1)
                    if og == 2:
                        norm_square(bi + 1)
                    if og == nog - 2:
                        norm_finish(bi + 1)
                if ok == "v":
                    vg = og - first["v"]
                    for jj in range(4):
                        pb = self.pcnt % 4
                        self.pcnt += 1
                        for c in range(NCH):
                            self.mm(PS[pb][:], HB[:, c, jj * 128:(jj + 1) * 128], W[:, c, :], c == 0, c == NCH - 1,
                                    reads=[(hk, c // 4), ("GB", gslot)], writes=[("PS", pb)])
                        eng = "dve" if jj % 2 == 0 else "act"
                        outap = VS[:, jj, vg * 512:(vg + 1) * 512]
                        if eng == "dve":
                            self.cp("dve", outap, PS[pb][:], reads=[("PS", pb)], writes=[("VS", vg)])
                        else:
                            self.act(outap, PS[pb][:], AF.Copy, reads=[("PS", pb)], writes=[("VS", vg)])
                        flush()
                    continue
                for f in range(4):
                    pb = self.pcnt % 4
                    self.pcnt += 1
                    for c in range(NCH):
                        self.mm(PS[pb][:], W[:, c, f * 128:(f + 1) * 128], HB[:, c, :], c == 0, c == NCH - 1,
                                reads=[(hk, c // 4), ("GB", gslot)], writes=[("PS", pb)])
                    u = (og - first[ok]) * 4 + f
                    if ok == "z":
                        oslot = 4 + self.ocnt % 4
                        self.ocnt += 1
                        self.act(SB[:, oslot, :], PS[pb][:], AF.Silu, reads=[("PS", pb)], writes=[("SB", oslot)])
                        self.dma("sp", "st_o%d" % oslot, self.zS[u, :, tok0:tok0 + 512], SB[:, oslot, :],
                                 reads=[("SB", oslot)])
                        flush()
                        continue
                    i2 = self.ucnt % 2
                    self.ucnt += 1
                    is_q = ok == "q"
                    Ct = TF[:, 4 if is_q else 6, :]
                    St = TF[:, 5 if is_q else 7, :]
                    self.act(SB[:, i2, :], PS[pb][:], AF.Square, reads=[("PS", pb)], writes=[("SB", i2)])
                    self.act(SB[:, 2 + i2, :], PS[pb][:], AF.Copy, reads=[("PS", pb)], writes=[("SB", 2 + i2)])
                    self.tt("dve", TF[:, 8 + i2, :], PS[pb][:], Ct, ALU.mult,
                            reads=[("PS", pb), ("TF", 4 if is_q else 6)], writes=[("TF", 8 + i2)])
                    flush()

                    def stage1(i2=i2, is_q=is_q, u=u, St=St, tok0=tok0, is_sample=is_sample):
                        m1, m2 = 5, 6 + i2
                        self.mm(PS[m1][:], self.ones_bf[:], SB[:, i2, :], True, True,
                                reads=[("SB", i2)], writes=[("PS", m1)])
                        self.mm(PS[m2][:], self.rmat[:, self.kind, :], SB[:, 2 + i2, :], True, True,
                                reads=[("SB", 2 + i2)], writes=[("PS", m2)])
                        self.act(TF[:, 10 + i2, :], PS[m1][:], AF.Ln, reads=[("PS", m1)], writes=[("TF", 10 + i2)],
                                 bias=self.epsc[:], scale=1.0 / 128)
                        self.act(TF[:, 12 + i2, :], TF[:, 10 + i2, :], AF.Exp, reads=[("TF", 10 + i2)],
                                 writes=[("TF", 12 + i2)], scale=-0.5)
                        self.tt("dve", TF[:, 14 + i2, :], PS[m2][:], St, ALU.mult,
                                reads=[("PS", m2), ("TF", 5 if is_q else 7)], writes=[("TF", 14 + i2)])
                        self.tt("pool", TF[:, 16 + i2, :], TF[:, 8 + i2, :], TF[:, 14 + i2, :], ALU.add,
                                reads=[("TF", 8 + i2), ("TF", 14 + i2)], writes=[("TF", 16 + i2)])
                        oslot = 4 + self.ocnt % 4
                        self.ocnt += 1
                        self.tt("pool", SB[:, oslot, :], TF[:, 16 + i2, :], TF[:, 12 + i2, :], ALU.mult,
                                reads=[("TF", 16 + i2), ("TF", 12 + i2)], writes=[("SB", oslot)])
                        if is_q:
                            self.dma("sp", "st_o%d" % oslot, self.qS[u, :, tok0:tok0 + 512], SB[:, oslot, :],
                                     reads=[("SB", oslot)])
                        elif is_sample:
                            t0 = tok0 - TP
                            self.dma("sp", "st_o%d" % oslot, kvsrc[0, u, :, t0:t0 + 512], SB[:, oslot, :],
                                     reads=[("SB", oslot)], writes=[("kvs", "k", u, t0)])
                            kv_sample_keys.append(("kvs", "k", u, t0))
                        else:
                            self.dma("sp", "st_o%d" % oslot, self.kvp[0, u, :, tok0:tok0 + 512], SB[:, oslot, :],
                                     reads=[("SB", oslot)])

                    pending.append(stage1)
            flush()
            nvg = NU // 4
            for u in range(NU):
                src = VS[:, :, u * 128:(u + 1) * 128]
                if is_sample:
                    t0 = tok0 - TP
                    dst = kvsrc[1, u, :, t0:t0 + 512].rearrange("p (j d) -> p j d", j=4)
                    self.dma("sp", "st_v", dst, src, reads=[("VS", g) for g in range(nvg)],
                             writes=[("kvs", "v", u, t0)])
                    kv_sample_keys.append(("kvs", "v", u, t0))
                else:
                    dst = self.kvp[1, u, :, tok0:tok0 + 512].rearrange("p (j d) -> p j d", j=4)
                    self.dma("sp", "st_v", dst, src, reads=[("VS", g) for g in range(nvg)])
            if bi == len(sample_blocks) - 1:
                semname = "ag%d" % self.li
                self.s.dma_nobarrier.add(semname)
                src2 = self.kvsrc[kind]
                dst2 = self.gath[kind]
                ag_op = self.s.op(
                    "pool",
                    lambda e, src2=src2, dst2=dst2: e.collective_compute(
                        "AllGather", ALU.bypass, replica_groups=[list(range(NCORES))], ins=[src2], outs=[dst2]),
                    reads=list(kv_sample_keys), writes=[("gath",)], dma=semname, ndma=1, dmainc=1)
        self.ag_op = ag_op
        s.barrier()

    def phase2(self):
        s = self.s
        kind, NU, T, TP, TS, KT = self.kind, self.NU, self.T, self.TP, self.TS, self.KT
        PS, SB, TF, XF, HB = self.PS, self.SB, self.TF, self.XF, self.HB
        NCK = KT // 128
        KB = [self.GB[0][:].rearrange("p a b -> p (a b)")[:, i * KT:(i + 1) * KT] for i in range(8192 // KT)][:2]
        vflat = [self.GB[1][:].rearrange("p a b -> p (a b)"), self.GB[2][:].rearrange("p a b -> p (a b)")]
        VB = []
        for fl in vflat:
            for i in range(min(2, 8192 // KT)):
                VB.append(fl[:, i * KT:(i + 1) * KT].rearrange("p (c d) -> p c d", d=128))
        nkb, nvb = len(KB), len(VB)
        gath = self.gath[kind].rearrange("(r a u p) t -> r a u p t", r=NCORES, a=2, u=NU)
        segs = [("P", 0, TP, TP // KT)]
        nr = max(1, KT // TS)
        ntile_s = (NCORES * TS) // KT
        segs.append(("S0", TP, TS, ntile_s))
        segs.append(("S1", TP + TS, TS, ntile_s))
        kcache = {}
        vcache = {}
        st = {"kn": 0, "vn": 0, "i": 0, "unit": 0}
        deferred = []

        def load_tile(which, seg, u, kt):
            cache = kcache if which == 0 else vcache
            tag = (seg[0], u, kt)
            for slot, tg in cache.items():
                if tg == tag:
                    return slot
            if which == 0:
                slot = st["kn"] % nkb
                st["kn"] += 1
                dst = KB[slot]
            else:
                slot = st["vn"] % nvb
                st["vn"] += 1
                dst = VB[slot]
            cache[slot] = tag
            key = ("KB" if which == 0 else "VB", slot)
            extra = ()
            if seg[0] == "P":
                src = self.kvp[which, u, :, kt * KT:(kt + 1) * KT]
                if which == 1:
                    src = src.rearrange("p (c d) -> p c d", d=128)
            else:
                sidx = 0 if seg[0] == "S0" else 1
                extra = (self.ag_op,)
                if KT >= TS:
                    r0 = kt * nr
                    src = gath[r0:r0 + nr, which, u, :, sidx * TS:(sidx + 1) * TS].rearrange("r p t -> p r t")
                    if which == 0:
                        dst = dst.rearrange("p (r t) -> p r t", r=nr)
                    else:
                        dst = dst.rearrange("p (r c) d -> p r (c d)", r=nr)
                else:
                    per = TS // KT
                    r0 = kt // per
                    o0 = sidx * TS + (kt % per) * KT
                    src = gath[r0, which, u, :, o0:o0 + KT]
                    if which == 1:
                        src = src.rearrange("p (c d) -> p c d", d=128)
            self.dma("sp", "ld_%s%d" % ("k" if which == 0 else "v", slot), dst, src, writes=[key], extra=extra)
            return slot

        def attn_unit(seg, ku, vus, qu, qb, fin):
            un = st["unit"]
            st["unit"] += 1
            oset = un % 2
            qtok = seg[1] + qb * 512
            qslot = un % 2
            self.dma("sp", "ld_q%d" % qslot, SB[:, qslot, :], self.qS[qu, :, qtok:qtok + 512], writes=[("SB", qslot)])
            nchunks = seg[3] * NCK
            pvq = []
            ci = 0
            dcnt = 0
            for kt in range(seg[3]):
                ks = load_tile(0, seg, ku, kt)
                vss = [load_tile(1, seg, vu, kt) for vu in vus]
                for c in range(NCK):
                    i = st["i"]
                    st["i"] += 1
                    sb_ = i % 3
                    pslot = 8 + i % 4
                    self.mm(PS[sb_][:], KB[ks][:, c * 128:(c + 1) * 128], SB[:, qslot, :], True, True,
                            reads=[("KB", ks), ("SB", qslot)], writes=[("PS", sb_)])
                    self.act(SB[:, pslot, :], PS[sb_][:], AF.Exp, reads=[("PS", sb_)], writes=[("SB", pslot)])
                    aslot = oset * 2 + dcnt % 2
                    if dcnt < 2:
                        self.cp("dve", TF[:, aslot, :], SB[:, pslot, :], reads=[("SB", pslot)], writes=[("TF", aslot)])
                    else:
                        self.tt("dve", TF[:, aslot, :], TF[:, aslot, :], SB[:, pslot, :], ALU.add,
                                reads=[("SB", pslot), ("TF", aslot)], writes=[("TF", aslot)])
                    dcnt += 1

                    def pv(ci=ci, pslot=pslot, vss=vss, c=c):
                        for vi, vs in enumerate(vss):
                            ob = 3 + oset * 2 + vi
                            self.mm(PS[ob][:], VB[vs][:, c, :], SB[:, pslot, :], ci == 0, ci == nchunks - 1,
                                    reads=[("VB", vs), ("SB", pslot)], writes=[("PS", ob)])

                    pvq.append(pv)
                    if len(pvq) > 2:
                        pvq.pop(0)()
                    if ci == 3:
                        while deferred:
                            deferred.pop(0)()
                    ci += 1
            while pvq:
                pvq.pop(0)()
            a0, a1 = oset * 2, oset * 2 + 1
            accs = 4 + oset
            self.tt("dve", TF[:, accs, :], TF[:, a0, :], TF[:, a1, :], ALU.add,
                    reads=[("TF", a0), ("TF", a1)], writes=[("TF", accs)])
            deferred.append(lambda: fin(oset, accs, qtok))

        def fin_common(oset, accs):
            if kind == 0:
                hi, lo = 2 + oset, 4 + oset
            else:
                hi, lo = 12, 13
            self.cp("pool", SB[:, hi, :], TF[:, accs, :], reads=[("TF", accs)], writes=[("SB", hi)])
            self.tt("pool", SB[:, lo, :], TF[:, accs, :], SB[:, hi, :], ALU.subtract,
                    reads=[("TF", accs), ("SB", hi)], writes=[("SB", lo)])
            self.mm(PS[7][:], self.ones_bf[:], SB[:, hi, :], True, False, reads=[("SB", hi)], writes=[("PS", 7)])
            self.mm(PS[7][:], self.ones_bf[:], SB[:, lo, :], False, True, reads=[("SB", lo)], writes=[("PS", 7)])
            rs = 6 + oset
            s.op("dve", lambda e: e.reciprocal(TF[:, rs, :], PS[7][:]), reads=[("PS", 7)], writes=[("TF", rs)])
            return rs

        zcnt = {"n": 0}

        def finA(h):
            def fin(oset, accs, qtok):
                rs = fin_common(oset, accs)
                zs = 12 + zcnt["n"] % 2
                asl = 14 + zcnt["n"] % 2
                zcnt["n"] += 1
                self.dma("sp", "ld_z%d" % zs, SB[:, zs, :], self.zS[h, :, qtok:qtok + 512], writes=[("SB", zs)])
                ob = 3 + oset * 2
                self.tt("dve", TF[:, 8 + oset, :], PS[ob][:], TF[:, rs, :], ALU.mult,
                        reads=[("PS", ob), ("TF", rs)], writes=[("TF", 8 + oset)])
                self.tt("pool", SB[:, asl, :], TF[:, 8 + oset, :], SB[:, zs, :], ALU.mult,
                        reads=[("TF", 8 + oset), ("SB", zs)], writes=[("SB", asl)])
                self.dma("sp", "st_a%d" % asl, self.aS[h, :, qtok:qtok + 512], SB[:, asl, :], reads=[("SB", asl)])
            return fin

        def finB(h, comp):
            def fin(oset, accs, qtok):
                rs = fin_common(oset, accs)
                if comp == 0:
                    for vi in range(2):
                        ob = 3 + oset * 2 + vi
                        self.tt("dve", XF[:, vi, :], PS[ob][:], TF[:, rs, :], ALU.mult,
                                reads=[("PS", ob), ("TF", rs)], writes=[("XF", ("s", vi))])
                    return
                zsl = []
                for vi in range(2):
                    zs = 4 + zcnt["n"] % 4
                    zcnt["n"] += 1
                    zsl.append(zs)
                    self.dma("sp", "ld_z%d" % zs, SB[:, zs, :], self.zS[2 * h + vi, :, qtok:qtok + 512],
                             writes=[("SB", zs)])
                for vi in range(2):
                    ob = 3 + oset * 2 + vi
                    self.tt("dve", TF[:, 8 + vi, :], PS[ob][:], TF[:, rs, :], ALU.mult,
                            reads=[("PS", ob), ("TF", rs)], writes=[("TF", 8 + vi)])
                    self.stt(XF[:, 2 + vi, :], TF[:, 8 + vi, :], self.dv[:, 4:5], XF[:, vi, :], ALU.mult, ALU.add,
                             reads=[("TF", 8 + vi), ("XF", ("s", vi))], writes=[("XF", ("d", vi))])
                    self.tt("pool", SB[:, 2 + vi, :], XF[:, 2 + vi, :], XF[:, 2 + vi, :], ALU.mult,
                            reads=[("XF", ("d", vi))], writes=[("SB", 2 + vi)])
                self.mm(PS[7][:], self.ones_bf[:], SB[:, 2, :], True, False, reads=[("SB", 2)], writes=[("PS", 7)])
                self.mm(PS[7][:], self.ones_bf[:], SB[:, 3, :], False, True, reads=[("SB", 3)], writes=[("PS", 7)])
                self.act(TF[:, 10, :], PS[7][:], AF.Ln, reads=[("PS", 7)], writes=[("TF", 10)],
                         bias=self.epsc[:], scale=1.0 / 256)
                self.act(TF[:, 11, :], TF[:, 10, :], AF.Exp, reads=[("TF", 10)], writes=[("TF", 11)], scale=-0.5)
                for vi in range(2):
                    self.tt("pool", TF[:, 12 + vi, :], XF[:, 2 + vi, :], TF[:, 11, :], ALU.mult,
                            reads=[("XF", ("d", vi)), ("TF", 11)], writes=[("TF", 12 + vi)])
                    asl = 14 + vi
                    self.stt(SB[:, asl, :], TF[:, 12 + vi, :], self.dv[:, 5 + vi:6 + vi], SB[:, zsl[vi], :],
                             ALU.mult, ALU.mult, reads=[("TF", 12 + vi), ("SB", zsl[vi])], writes=[("SB", asl)])
                    self.dma("sp", "st_a%d" % asl, self.aS[2 * h + vi, :, qtok:qtok + 512], SB[:, asl, :],
                             reads=[("SB", asl)])
            return fin

        for seg in segs:
            nqb = seg[2] // 512
            if kind == 0:
                for g in range(4):
                    for hq in range(4):
                        h = 4 * g + hq
                        for qb in range(nqb):
                            attn_unit(seg, g, [g], h, qb, finA(h))
            else:
                for h in range(8):
                    for qb in range(nqb):
                        for comp in range(2):
                            attn_unit(seg, 2 * h + comp, [2 * h, 2 * h + 1], 2 * h + comp, qb, finB(h, comp))
        while deferred:
            deferred.pop(0)()
        s.barrier()

    def phase3(self, last):
        s = self.s
        XF, HB, PS, TF = self.XF, self.HB, self.PS, self.TF
        T = self.T
        wc = 0
        pc = 0
        tcnt = 0
        wseq3 = [(b, og) for b in range(self.NB) for og in range(4)]
        wi = {"n": 0}

        def issue_w3(upto):
            while wi["n"] <= upto and wi["n"] < len(wseq3):
                n = wi["n"]
                gslot = n % 3
                self.dma("sp", "ld_g%d" % gslot, self.GB[gslot][:], self.wb_out[wseq3[n][1]], writes=[("GB", gslot)])
                wi["n"] += 1

        issue_w3(1)
        HBs = [self.HB, self.HB2]
        xbufs = [self.XF[:], self.XF[:]] if last else [self.XF[:], TF[:, 0:16, :]]

        def load_blk(b):
            t0 = b * 512
            pb_ = b % 2
            self.dma("sp", "ld_a%d" % pb_, HBs[pb_][:], self.aS[:, :, t0:t0 + 512].rearrange("c p t -> p c t"),
                     writes=[("HB", pb_)])
            xk = 0 if last else pb_
            self.dma("sp", "ld_x%d" % xk, xbufs[pb_],
                     self.xT[:, :, t0:t0 + 512].rearrange("c p t -> p c t"),
                     writes=[("XF", xk, g) for g in range(16)])

        load_blk(0)
        for b in range(self.NB):
            tok0 = b * 512
            HB = HBs[b % 2]
            hkey = ("HB", b % 2)
            XF = xbufs[b % 2]
            xk = 0 if last else b % 2
            if b + 1 < self.NB and not last:
                load_blk(b + 1)
            for og in range(4):
                gslot = wc % 3
                issue_w3(wc + 2)
                wc += 1
                W = self.GB[gslot]
                for f in range(4):
                    fo = og * 4 + f
                    pb = pc % 4
                    pc += 1
                    for c in range(NCH):
                        self.mm(PS[pb][:], W[:, c, f * 128:(f + 1) * 128], HB[:, c, :], c == 0, c == NCH - 1,
                                reads=[hkey, ("GB", gslot)], writes=[("PS", pb)])
                    self.tt("dve", XF[:, fo, :], PS[pb][:], XF[:, fo, :], ALU.add,
                            reads=[("PS", pb), ("XF", xk, fo)], writes=[("XF", xk, fo)])
            if not last:
                self.dma("sp", "st_x%d" % xk, self.xT[:, :, tok0:tok0 + 512].rearrange("c p t -> p c t"),
                         XF, reads=[("XF", xk, g) for g in range(16)])
            else:
                for j in range(4):
                    ys = j % 2
                    ysl = TF[:, ys * 4:(ys + 1) * 4, :].rearrange("p a b -> p (a b)")
                    for fg in range(4):
                        pb = 4 + tcnt % 4
                        ps = PS[pb]
                        for i in range(4):
                            fo = fg * 4 + i
                            s.op("pe", lambda e, ps=ps, i=i, fo=fo, j=j: e.transpose(
                                ps[:, i * 128:(i + 1) * 128], XF[:, fo, j * 128:(j + 1) * 128], self.ident[:]),
                                reads=[("XF", xk, fo)], writes=[("PS", pb)])
                        eng = "dve" if tcnt % 2 == 0 else "act"
                        outap = ysl[:, fg * 512:(fg + 1) * 512]
                        if eng == "dve":
                            self.cp("dve", outap, ps[:], reads=[("PS", pb)], writes=[("YS", ys)])
                        else:
                            self.act(outap, ps[:], AF.Copy, reads=[("PS", pb)], writes=[("YS", ys)])
                        tcnt += 1
                    self.dma("sp", "st_y%d" % ys, self.y[tok0 + j * 128: tok0 + (j + 1) * 128, :], ysl,
                             reads=[("YS", ys)])
                if b + 1 < self.NB:
                    load_blk(b + 1)
        s.barrier()


def _rope_tables(TP, TS, core):
    T = TP + 2 * TS
    pos = np.concatenate([np.arange(TP), core * TS + np.arange(TS), core * TS + np.arange(TS)]).astype(np.float32)
    out = np.zeros((2, 2, 128, T), np.float32)
    inv64 = (1.0 / (10000.0 ** (np.arange(0, 64, 2, dtype=np.float32) / 64))).astype(np.float32)
    row = np.floor(pos / 64.0).astype(np.float32)
    col = (pos - row * 64.0).astype(np.float32)
    angr = (row[None, :] * inv64[:, None]).astype(np.float32)
    angc = (col[None, :] * inv64[:, None]).astype(np.float32)
    out[0, 0, 0:32] = np.cos(angr); out[0, 0, 32:64] = np.cos(angr)
    out[0, 0, 64:96] = np.cos(angc); out[0, 0, 96:128] = np.cos(angc)
    out[0, 1, 0:32] = np.sin(angr); out[0, 1, 32:64] = np.sin(angr)
    out[0, 1, 64:96] = np.sin(angc); out[0, 1, 96:128] = np.sin(angc)
    inv128 = (1.0 / (10000.0 ** (np.arange(0, 128, 2, dtype=np.float32) / 128))).astype(np.float32)
    ang = (pos[None, :] * inv128[:, None]).astype(np.float32)
    out[1, 0, 0:64] = np.cos(ang); out[1, 0, 64:128] = np.cos(ang)
    out[1, 1, 0:64] = np.sin(ang); out[1, 1, 64:128] = np.sin(ang)
    return out


def _rmats():
    r = np.zeros((2, 128, 128), np.float32)
    for base in (0, 64):
        for m in range(32):
            r[0, base + m + 32, base + m] = -1.0
            r[0, base + m, base + m + 32] = 1.0
    for m in range(64):
        r[1, m + 64, m] = -1.0
        r[1, m, m + 64] = 1.0
    return r.astype(ml_dtypes.bfloat16)


def _perm(kind):
    p = np.zeros(128, np.int64)
    if kind == 0:
        for base in (0, 64):
            for m in range(32):
                p[base + m] = base + m + 32
                p[base + m + 32] = base + m
    else:
        for m in range(64):
            p[m] = m + 64
            p[m + 64] = m
    return p


def _vecs(inp, depth):
    v = np.zeros((128, 4 * VCOLS), np.float32)
    for li in range(depth):
        kind, j = li % 2, li // 2
        b = li * VCOLS
        pre = "a_" if kind == 0 else "b_"
        v[:, b:b + 16] = np.asarray(inp[pre + "norm"][j], np.float32).reshape(16, 128).T
        gq = np.asarray(inp[pre + "q_norm"][j], np.float32)
        gk = np.asarray(inp[pre + "k_norm"][j], np.float32)
        pm = _perm(kind)
        v[:, b + 16] = gq
        v[:, b + 17] = gq[pm]
        v[:, b + 18] = gk
        v[:, b + 19] = gk[pm]
        if kind == 1:
            v[:, b + 20] = np.asarray(inp["b_lambda_q1"][j], np.float32)
            v[:, b + 21] = np.asarray(inp["b_lambda_k1"][j], np.float32)
            v[:, b + 22] = np.asarray(inp["b_lambda_q2"][j], np.float32)
            v[:, b + 23] = np.asarray(inp["b_lambda_k2"][j], np.float32)
            gs = np.asarray(inp["b_subln"][j], np.float32)
            v[:, b + 24] = gs[0:128]
            v[:, b + 25] = gs[128:256]
    return v


_PROG_CACHE = {}


def _get_prog(cfg):
    key = tuple(sorted(cfg.items()))
    if key not in _PROG_CACHE:
        _PROG_CACHE[key] = Prog(dict(cfg)).build()
    return _PROG_CACHE[key]


def run_cfg(inp, cfg, trace=False):
    TP, TS, depth = cfg["TP"], cfg["TS"], cfg["depth"]
    xp = np.asarray(inp["x_prompt"], np.float32)
    xs = np.asarray(inp["x_sample"], np.float32)
    assert xp.shape == (NCORES, TP, D) and xs.shape == (2, NCORES * TS, D)
    nc = _get_prog(cfg)
    vecs = _vecs(inp, depth)
    rm = _rmats()
    ident = np.eye(128, dtype=np.float32)
    common = {
        "a_w_in": np.ascontiguousarray(np.asarray(inp["a_w_in"], np.float32)),
        "a_w_out": np.ascontiguousarray(np.asarray(inp["a_w_out"], np.float32)),
        "b_w_in": np.ascontiguousarray(np.asarray(inp["b_w_in"], np.float32)),
        "b_w_out": np.ascontiguousarray(np.asarray(inp["b_w_out"], np.float32)),
        "vecs": vecs, "ident": ident, "rmat": rm,
    }
    in_maps = []
    for c in range(NCORES):
        xin = np.concatenate([xp[c], xs[0, c * TS:(c + 1) * TS], xs[1, c * TS:(c + 1) * TS]], axis=0)
        m = dict(common)
        m["xin"] = np.ascontiguousarray(xin)
        m["rope"] = _rope_tables(TP, TS, c)
        in_maps.append(m)
    res = run_bass_kernel_spmd(nc, in_maps, core_ids=list(range(NCORES)), trace=trace)
    yp = np.zeros((NCORES, TP, D), np.float32)
    ys = np.zeros((2, NCORES * TS, D), np.float32)
    for c in range(NCORES):
        y = np.asarray(res.results[c]["y"], np.float32)
        yp[c] = y[0:TP]
        ys[0, c * TS:(c + 1) * TS] = y[TP:TP + TS]
        ys[1, c * TS:(c + 1) * TS] = y[TP + TS:TP + 2 * TS]
    return (yp, ys), res


def kernel(**inputs):
    cfg = {"TP": 4096, "TS": 1024, "KT": 4096, "depth": 4}
    out, _ = run_cfg(inputs, cfg)
    return out
```
